# Optimizing a Trainium2 kernel written in Bass

```python
import math, functools
import jax, jax.numpy as jnp
from jax import lax
import numpy as np

D_MODEL = 1024
BATCH = 32
SEQ = 2048
DEPTH = 1

CTX_LEN = 256
GRID_W = 64
M_HEADS = 4
M_WIDTH = D_MODEL
M_HEAD_DIM = M_WIDTH // M_HEADS
CHUNK = 64
H_WIDTH = D_MODEL
HY_ORDER = 2
HY_EMB = 33
HY_BANDS = (HY_EMB - 1) // 2
HY_FILTER_WIDTH = 64
DECAY_TARGET = 1e-2
FAST_DECAY_PCT = 0.3
SLOW_DECAY_PCT = 1.5
SHORT_CONV = 3
D_FF = ((8 * D_MODEL + 3 * 256 - 1) // (3 * 256)) * 256
EPS = 1e-6
NEG = -1e30

K_OFF = 0
V_OFF = M_WIDTH
G_OFF = 2 * M_WIDTH
Q_OFF = G_OFF + 4 * M_HEADS
O_OFF = Q_OFF + M_WIDTH
HY_OFF = O_OFF + M_WIDTH
MG_OFF = HY_OFF + 3 * H_WIDTH
N_IN = MG_OFF + 2 * D_MODEL
R_O = O_OFF - Q_OFF
R_HY = HY_OFF - Q_OFF
R_GM = MG_OFF - Q_OFF
R_GH = R_GM + D_MODEL

kernel_name = "hybrid_mlstm_hyena_diffusion_block"


def rmsnorm(x, g):
    xf = x.astype(jnp.float32)
    y = xf * lax.rsqrt(jnp.mean(xf * xf, axis=-1, keepdims=True) + EPS)
    return (y * g.astype(jnp.float32)).astype(x.dtype)


def seq_conv(u, w, b):
    ch = u.shape[-1]
    y = lax.conv_general_dilated(
        u, w.astype(u.dtype)[:, None, :], window_strides=(1,),
        padding=((SHORT_CONV // 2, SHORT_CONV // 2),),
        dimension_numbers=("NWC", "WIO", "NWC"), feature_group_count=ch)
    return y + b.astype(u.dtype)


def grid_conv(u, w, b, rows):
    bsz, n, ch = u.shape
    return seq_conv(u.reshape(bsz * rows, GRID_W, ch), w, b).reshape(bsz, n, ch)


def to_heads(u):
    bsz, n, _ = u.shape
    return u.reshape(bsz, n, M_HEADS, M_HEAD_DIM).transpose(0, 2, 1, 3)


def flip(u):
    return jnp.flip(u, axis=2)


def project_kvg(h, lp, conv_fn):
    f32 = jnp.float32
    p = h @ lp["w_in"][:, :Q_OFF] + lp["b_in"][:Q_OFF]
    k = jax.nn.silu(conv_fn(p[..., K_OFF:V_OFF], lp["kq_conv_w"][:, :M_WIDTH], lp["kq_conv_b"][:M_WIDTH]))
    kh = to_heads(k.astype(f32)) * (M_HEAD_DIM ** -0.5)
    vh = to_heads(p[..., V_OFF:G_OFF].astype(f32))
    bsz, n, _ = h.shape
    g = p[..., G_OFF:Q_OFF].astype(f32).reshape(bsz, n, 4, M_HEADS).transpose(2, 0, 3, 1)
    lg = (g[0], jax.nn.log_sigmoid(g[1]), g[2], jax.nn.log_sigmoid(g[3]))
    return kh, vh, lg


def mlstm_final_state(k, v, log_i, log_f):
    b = jnp.cumsum(log_f, axis=-1)
    b_end = b[..., -1]
    logw = b_end[..., None] - b + log_i
    m = jnp.maximum(b_end, jnp.max(logw, axis=-1))
    w = jnp.exp(logw - m[..., None])
    C = jnp.einsum("bhsv,bhsk->bhvk", w[..., None] * v, k)
    n = jnp.einsum("bhs,bhsk->bhk", w, k)
    return C, n, m


def mlstm_chunkwise(q, k, v, log_i, log_f, C0, n0, m0):
    bsz, nh, n, dh = q.shape
    nc = n // CHUNK

    def split(u):
        return jnp.moveaxis(u.reshape(u.shape[:2] + (nc, CHUNK) + u.shape[3:]), 2, 0)

    tril = jnp.tril(jnp.ones((CHUNK, CHUNK), dtype=bool))

    def step(carry, blk):
        C, nv, m = carry
        qb, kb, vb, ib, fb = blk
        b = jnp.cumsum(fb, axis=-1)
        logD = jnp.where(tril, b[..., :, None] - b[..., None, :] + ib[..., None, :], NEG)
        inter = b + m[..., None]
        m_t = jnp.maximum(inter, jnp.max(logD, axis=-1))
        s = jnp.einsum("bhtd,bhsd->bhts", qb, kb) * jnp.exp(logD - m_t[..., None])
        a = jnp.exp(inter - m_t)
        num = jnp.einsum("bhts,bhsd->bhtd", s, vb) + a[..., None] * jnp.einsum("bhvk,bhtk->bhtv", C, qb)
        den = jnp.sum(s, axis=-1) + a * jnp.einsum("bhk,bhtk->bht", nv, qb)
        hb = num / jnp.maximum(jnp.abs(den), jnp.exp(-m_t))[..., None]
        m_new = m_t[..., -1]
        w = jnp.exp(b[..., -1:] - b + ib - m_new[..., None])
        decay = jnp.exp(b[..., -1] + m - m_new)
        C = decay[..., None, None] * C + jnp.einsum("bhsv,bhsk->bhvk", w[..., None] * vb, kb)
        nv = decay[..., None] * nv + jnp.einsum("bhs,bhsk->bhk", w, kb)
        return (C, nv, m_new), hb

    _, hs = lax.scan(step, (C0, n0, m0), (split(q), split(k), split(v), split(log_i), split(log_f)))
    return jnp.moveaxis(hs, 0, 2).reshape(bsz, nh, n, dh)


def zero_state(bsz):
    f32 = jnp.float32
    return (jnp.zeros((bsz, M_HEADS, M_HEAD_DIM, M_HEAD_DIM), f32),
            jnp.zeros((bsz, M_HEADS, M_HEAD_DIM), f32),
            jnp.zeros((bsz, M_HEADS), f32))


def context_states(kh, vh, lg):
    li_f, lf_f, li_b, lf_b = lg
    st_f = mlstm_final_state(kh, vh, li_f, lf_f)
    st_b = mlstm_final_state(flip(kh), flip(vh), flip(li_b), flip(lf_b))
    return st_f, st_b


def head_norm(hh, g):
    mu = jnp.mean(hh, axis=-1, keepdims=True)
    var = jnp.mean(jnp.square(hh - mu), axis=-1, keepdims=True)
    y = (hh - mu) * lax.rsqrt(var + EPS)
    bsz, _, n, _ = hh.shape
    return y.transpose(0, 2, 1, 3).reshape(bsz, n, M_WIDTH) * g.astype(jnp.float32)


def hyena_spectra(n, lp):
    f32 = jnp.float32
    t = jnp.linspace(0.0, 1.0, n, dtype=f32)[:, None]
    bands = jnp.linspace(1e-4, HY_BANDS - 1, HY_BANDS, dtype=f32)
    ang = (2.0 * math.pi / n) * jnp.arange(n, dtype=f32)[:, None] * bands[None, :]
    z = jnp.concatenate([t, jnp.cos(ang), -jnp.sin(ang)], axis=-1)
    fr = lp["hy_freq"].astype(f32)
    a = jnp.sin(fr * (z @ lp["hy_w1"].astype(f32) + lp["hy_b1"].astype(f32)))
    a = jnp.sin(fr * (a @ lp["hy_w2"].astype(f32) + lp["hy_b2"].astype(f32)))
    a = jnp.sin(fr * (a @ lp["hy_w3"].astype(f32) + lp["hy_b3"].astype(f32)))
    window = jnp.exp(-t[:, :, None, None] * jnp.abs(lp["hy_decay"].astype(f32)))
    k = (a @ lp["hy_w4"].astype(f32)).reshape(n, HY_ORDER, 2, H_WIDTH) * window
    kf, kb = k[:, :, 0], k[:, :, 1]
    circ = jnp.concatenate([kf[:1] + kb[:1], kf[1:], jnp.zeros_like(kf[:1]), kb[:0:-1]], axis=0)
    return jnp.fft.rfft(jnp.moveaxis(circ, 1, 0), axis=1)


def fftconv(u, k_spec):
    n = u.shape[1]
    U = jnp.fft.rfft(u, n=2 * n, axis=1)
    return jnp.fft.irfft(U * k_spec, n=2 * n, axis=1)[:, :n]


def hyena(v, gates, spec, skip):
    z = v
    for o in range(HY_ORDER):
        z = gates[o] * (fftconv(z, spec[o]) + skip[o].astype(jnp.float32) * z)
    return z


def mixer(h, kh, vh, lg, st_f, st_b, lp, conv_fn, spec):
    f32 = jnp.float32
    p = h @ lp["w_in"][:, Q_OFF:] + lp["b_in"][Q_OFF:]
    q = jax.nn.silu(conv_fn(p[..., :R_O], lp["kq_conv_w"][:, M_WIDTH:], lp["kq_conv_b"][M_WIDTH:]))
    o = jax.nn.sigmoid(p[..., R_O:R_HY]).astype(f32)
    hy = conv_fn(p[..., R_HY:R_GM], lp["hy_conv_w"], lp["hy_conv_b"]).astype(f32)
    gm = jax.nn.sigmoid(p[..., R_GM:R_GH])
    gh = jax.nn.sigmoid(p[..., R_GH:])
    qh = to_heads(q.astype(f32))
    li_f, lf_f, li_b, lf_b = lg
    h_f = mlstm_chunkwise(qh, kh, vh, li_f, lf_f, *st_f)
    h_b = flip(mlstm_chunkwise(flip(qh), flip(kh), flip(vh), flip(li_b), flip(lf_b), *st_b))
    hm = head_norm(h_f + h_b, lp["m_norm_g"]) * o
    hv, hx1, hx2 = jnp.split(hy, 3, axis=-1)
    hh = hyena(hv, (hx1, hx2), spec, lp["hy_skip"])
    y = gm * (hm.astype(h.dtype) @ lp["w_pm"]) + gh * (hh.astype(h.dtype) @ lp["w_ph"])
    return y @ lp["w_out"]


def swiglu(h, w1, w3, w2):
    return (jax.nn.silu(h @ w1) * (h @ w3)) @ w2


def setup_inputs(seed: int = 0) -> dict:
    key = jax.random.key(seed)
    ks = iter(jax.random.split(key, 48))
    f32 = jnp.float32

    def nrm(shape, scale):
        return jax.random.normal(next(ks), shape, f32) * scale

    def uni(shape, lo, hi):
        return jax.random.uniform(next(ks), shape, f32, lo, hi)

    def gain(shape):
        return 1.0 + nrm(shape, 0.02)

    L = DEPTH
    D = D_MODEL
    b_in = jnp.concatenate([
        nrm((L, G_OFF), 0.02),
        nrm((L, M_HEADS), 0.1), uni((L, M_HEADS), 3.0, 6.0),
        nrm((L, M_HEADS), 0.1), uni((L, M_HEADS), 3.0, 6.0),
        nrm((L, N_IN - Q_OFF), 0.02)], axis=-1)
    decay_lo = -math.log(DECAY_TARGET) / SLOW_DECAY_PCT
    decay_hi = -math.log(DECAY_TARGET) / FAST_DECAY_PCT
    return {
        "x": nrm((BATCH, SEQ, D), 1.0),
        "c": nrm((BATCH, D), 1.0),
        "ctx": nrm((BATCH, CTX_LEN, D), 1.0),
        "c_ctx": nrm((D,), 1.0),
        "w_ada": nrm((L, D, 6 * D), 0.5 * D ** -0.5),
        "b_ada": nrm((L, 6 * D), 0.02),
        "norm1_g": gain((L, D)),
        "w_in": nrm((L, D, N_IN), D ** -0.5),
        "b_in": b_in,
        "kq_conv_w": nrm((L, SHORT_CONV, 2 * M_WIDTH), SHORT_CONV ** -0.5),
        "kq_conv_b": nrm((L, 2 * M_WIDTH), 0.02),
        "m_norm_g": gain((L, M_WIDTH)),
        "hy_conv_w": nrm((L, SHORT_CONV, 3 * H_WIDTH), SHORT_CONV ** -0.5),
        "hy_conv_b": nrm((L, 3 * H_WIDTH), 0.02),
        "hy_w1": nrm((L, HY_EMB, HY_FILTER_WIDTH), HY_EMB ** -0.5),
        "hy_b1": nrm((L, HY_FILTER_WIDTH), 0.02),
        "hy_w2": nrm((L, HY_FILTER_WIDTH, HY_FILTER_WIDTH), HY_FILTER_WIDTH ** -0.5),
        "hy_b2": nrm((L, HY_FILTER_WIDTH), 0.02),
        "hy_w3": nrm((L, HY_FILTER_WIDTH, HY_FILTER_WIDTH), HY_FILTER_WIDTH ** -0.5),
        "hy_b3": nrm((L, HY_FILTER_WIDTH), 0.02),
        "hy_w4": nrm((L, HY_FILTER_WIDTH, HY_ORDER * 2 * H_WIDTH), 0.05 * HY_FILTER_WIDTH ** -0.5),
        "hy_freq": 1.0 + nrm((L, HY_FILTER_WIDTH), 0.1),
        "hy_decay": uni((L, HY_ORDER, 2, H_WIDTH), decay_lo, decay_hi),
        "hy_skip": nrm((L, HY_ORDER, H_WIDTH), 0.5),
        "w_pm": nrm((L, M_WIDTH, D), M_WIDTH ** -0.5),
        "w_ph": nrm((L, H_WIDTH, D), H_WIDTH ** -0.5),
        "w_out": nrm((L, D, D), D ** -0.5),
        "norm2_g": gain((L, D)),
        "ffn_w1": nrm((L, D, D_FF), D ** -0.5),
        "ffn_w3": nrm((L, D, D_FF), D ** -0.5),
        "ffn_w2": nrm((L, D_FF, D), D_FF ** -0.5),
        "final_g": gain((D,)),
    }


def reference(x, c, ctx, c_ctx, w_ada, b_ada, norm1_g, w_in, b_in, kq_conv_w, kq_conv_b,
              m_norm_g, hy_conv_w, hy_conv_b, hy_w1, hy_b1, hy_w2, hy_b2, hy_w3, hy_b3,
              hy_w4, hy_freq, hy_decay, hy_skip, w_pm, w_ph, w_out, norm2_g,
              ffn_w1, ffn_w3, ffn_w2, final_g):
    rows = x.shape[1] // GRID_W
    lat_conv = functools.partial(grid_conv, rows=rows)
    seq_len = x.shape[1]
    ctx_len = ctx.shape[1]
    silu_c = jax.nn.silu(c)
    silu_cc = jax.nn.silu(c_ctx)
    ctx_s = ctx
    for l in range(DEPTH):
        last = l == DEPTH - 1
        lp = {
            "w_in": w_in[l], "b_in": b_in[l], "kq_conv_w": kq_conv_w[l], "kq_conv_b": kq_conv_b[l],
            "m_norm_g": m_norm_g[l], "hy_conv_w": hy_conv_w[l], "hy_conv_b": hy_conv_b[l],
            "hy_w1": hy_w1[l], "hy_b1": hy_b1[l], "hy_w2": hy_w2[l], "hy_b2": hy_b2[l],
            "hy_w3": hy_w3[l], "hy_b3": hy_b3[l], "hy_w4": hy_w4[l], "hy_freq": hy_freq[l],
            "hy_decay": hy_decay[l], "hy_skip": hy_skip[l],
            "w_pm": w_pm[l], "w_ph": w_ph[l], "w_out": w_out[l],
        }
        mod = (silu_c @ w_ada[l] + b_ada[l])[:, None, :]
        sh1, sc1, g1, sh2, sc2, g2 = jnp.split(mod, 6, axis=-1)
        modc = silu_cc @ w_ada[l] + b_ada[l]
        csh1, csc1, cg1, csh2, csc2, cg2 = jnp.split(modc, 6, axis=-1)
        hc = rmsnorm(ctx_s, norm1_g[l]) * (1.0 + csc1) + csh1
        khc, vhc, lgc = project_kvg(hc, lp, seq_conv)
        st_f, st_b = context_states(khc, vhc, lgc)
        h = rmsnorm(x, norm1_g[l]) * (1.0 + sc1) + sh1
        kh, vh, lg = project_kvg(h, lp, lat_conv)
        y = mixer(h, kh, vh, lg, st_f, st_b, lp, lat_conv, hyena_spectra(seq_len, lp))
        if not last:
            zero = zero_state(ctx_s.shape[0])
            yc = mixer(hc, khc, vhc, lgc, zero, zero, lp, seq_conv, hyena_spectra(ctx_len, lp))
            ctx_s = ctx_s + cg1 * yc
            hc2 = rmsnorm(ctx_s, norm2_g[l]) * (1.0 + csc2) + csh2
            ctx_s = ctx_s + cg2 * swiglu(hc2, ffn_w1[l], ffn_w3[l], ffn_w2[l])
        x = x + g1 * y
        h2 = rmsnorm(x, norm2_g[l]) * (1.0 + sc2) + sh2
        x = x + g2 * swiglu(h2, ffn_w1[l], ffn_w3[l], ffn_w2[l])
    return rmsnorm(x, final_g)
```

```python
import math
from contextlib import ExitStack
import numpy as np
import ml_dtypes
import concourse.bass as bass
import concourse.mybir as mybir
from concourse.bass_utils import run_bass_kernel_spmd

F32 = mybir.dt.float32
BF16 = mybir.dt.bfloat16
AF = mybir.ActivationFunctionType
ALU = mybir.AluOpType
AX = mybir.AxisListType

D = 1024
L = 2048
CL = 256
NH = 4
DH = 256
DFF = 2816
NFC = DFF // 128
NIN = 9232
K_OFF, V_OFF, G_OFF = 0, 1024, 2048
Q_OFF = G_OFF + 16
O_OFF = Q_OFF + 1024
HY_OFF = O_OFF + 1024
MG_OFF = HY_OFF + 3072
EPS = 1e-6
NFFT = 4096
NCORES = 8
NB_FULL = 4


class SemSlot:
    __slots__ = ("sem", "count")

    def __init__(self):
        self.sem = None
        self.count = 0


class Buf:
    __slots__ = ("name", "last_w", "readers", "slot", "epoch", "last_dma")

    def __init__(self, name):
        self.name = name
        self.last_w = None
        self.readers = []
        self.slot = None
        self.epoch = -1
        self.last_dma = None


class Op:
    __slots__ = ("eng", "fn", "deps", "is_dma", "owner", "sig", "idx", "group")

    def __init__(self, eng, fn, is_dma=False, owner=None):
        self.eng = eng
        self.fn = fn
        self.deps = set()
        self.is_dma = is_dma
        self.owner = owner
        self.sig = None
        self.idx = None
        self.group = None


class Prog:
    ENGS = ("pe", "act", "dve", "pool", "sp")

    def __init__(self, nc):
        self.nc = nc
        self.ops = []
        self.last_on = {e: None for e in self.ENGS}
        self.live_dma = {}
        self.deferred_dma = {}
        self.bar = None
        self.bar_seen = set()
        self.epoch = 0
        self.free_slots = {e: [] for e in self.ENGS}
        self.used_slots = []
        self.all_slots = []

    def buf(self, name="b"):
        return Buf(name)

    def _add(self, op, reads, writes):
        op.idx = len(self.ops)
        self.ops.append(op)
        for b in reads:
            if b.last_w is not None:
                op.deps.add(b.last_w)
        for b in writes:
            if b.last_w is not None:
                op.deps.add(b.last_w)
            op.deps.update(b.readers)
        for b in reads:
            b.readers.append(op.idx)
        for b in writes:
            b.last_w = op.idx
            b.readers = []
        op.deps.discard(op.idx)
        if self.bar is not None and op.eng not in self.bar_seen:
            op.deps.add(self.bar)
            self.bar_seen.add(op.eng)
        if not op.is_dma:
            self.last_on[op.eng] = op.idx
        return op

    def op(self, eng, fn, reads=(), writes=()):
        return self._add(Op(eng, fn), reads, writes)

    def mm(self, fn, reads=(), writes=()):
        return self._add(Op("pe", fn), reads, writes)

    def dma(self, eng, fn, owner, reads=(), writes=(), deferred=False, group=None):
        o = Op(eng, fn, is_dma=True, owner=owner)
        o.group = group
        self._add(o, reads, writes)
        if owner.slot is None:
            owner.slot = {}
        if deferred:
            if eng not in owner.slot:
                sl = SemSlot()
                self.all_slots.append(sl)
                owner.slot[eng] = (sl, -1)
        elif eng not in owner.slot or owner.slot[eng][1] != self.epoch:
            if self.free_slots[eng]:
                sl = self.free_slots[eng].pop()
            else:
                sl = SemSlot()
                self.all_slots.append(sl)
            self.used_slots.append((eng, sl))
            owner.slot[eng] = (sl, self.epoch)
        if owner.epoch != self.epoch and not deferred:
            owner.epoch = self.epoch
            owner.last_dma = None
        slot = owner.slot[eng][0]
        if owner.last_dma is not None:
            prev = self.ops[owner.last_dma]
            if group is not None and prev.group == group:
                o.deps.discard(prev.idx)
                o.deps.update(prev.deps)
            else:
                o.deps.add(owner.last_dma)
        owner.last_dma = o.idx
        slot.count += 1
        o.sig = (slot, 16 * slot.count)
        if deferred:
            self.deferred_dma[(id(owner), eng)] = o.idx
        else:
            self.live_dma[(id(owner), eng)] = o.idx
        return o

    def barrier(self, include_deferred=False):
        if include_deferred:
            self.live_dma.update(self.deferred_dma)
            self.deferred_dma = {}
        o = Op("sp", lambda e: e.nop())
        o.idx = len(self.ops)
        self.ops.append(o)
        for e in self.ENGS:
            if self.last_on[e] is not None:
                o.deps.add(self.last_on[e])
        o.deps.update(self.live_dma.values())
        if self.bar is not None:
            o.deps.add(self.bar)
        self.live_dma = {}
        for e_, sl_ in self.used_slots:
            self.free_slots[e_].append(sl_)
        self.used_slots = []
        self.epoch += 1
        self.bar = o.idx
        self.bar_seen = {"sp"}
        self.last_on["sp"] = o.idx
        return o

    def emit(self, final_wait_ops=()):
        nc = self.nc
        ops = self.ops
        needed = set()
        for o in ops:
            needed.update(o.deps)
        for o in final_wait_ops:
            needed.add(o.idx)
        eng_cnt = {e: 0 for e in self.ENGS}
        dma_bufs = self.all_slots
        for o in ops:
            if o.is_dma:
                pass
            elif o.idx in needed:
                eng_cnt[o.eng] += 1
                o.sig = (o.eng, eng_cnt[o.eng])
        self.n_sems = len(dma_bufs) + len(self.ENGS)
        with ExitStack() as st:
            esem = {e: st.enter_context(nc.semaphore("s_" + e)) for e in self.ENGS}
            for i, b in enumerate(dma_bufs):
                b.sem = st.enter_context(nc.semaphore("d%d" % i))
            block = st.enter_context(nc.Block())

            def sem_of(key):
                return esem[key] if isinstance(key, str) else key.sem

            per_eng = {e: [o for o in ops if o.eng == e] for e in self.ENGS}
            final = list(final_wait_ops)

            def run_engine(ename, eng):
                waited = {}
                for o in per_eng[ename]:
                    need = {}
                    for d in o.deps:
                        do = ops[d]
                        if do.sig is None:
                            continue
                        if (not do.is_dma) and do.eng == ename and ename in ("pe", "sp"):
                            continue
                        key, val = do.sig
                        kid = key if isinstance(key, str) else id(key)
                        if need.get(kid, (None, 0))[1] < val:
                            need[kid] = (key, val)
                    for kid, (key, val) in need.items():
                        if waited.get(kid, 0) >= val:
                            continue
                        eng.wait_ge(sem_of(key), val)
                        waited[kid] = val
                    ins = o.fn(eng)
                    if o.sig is not None:
                        key, val = o.sig
                        ins.then_inc(sem_of(key), 16 if o.is_dma else 1)
                if ename == "sp":
                    for o in final:
                        key, val = o.sig
                        eng.wait_ge(sem_of(key), val)

            block.sync(lambda e: run_engine("sp", e))
            block.tensor(lambda e: run_engine("pe", e))
            block.scalar(lambda e: run_engine("act", e))
            block.vector(lambda e: run_engine("dve", e))
            block.gpsimd(lambda e: run_engine("pool", e))


class Ring:
    def __init__(self, views):
        self.items = [(v, Buf("r")) for v in views]
        self.i = 0

    def get(self):
        it = self.items[self.i % len(self.items)]
        self.i += 1
        return it


COLV = {}
_o = 0
for _n, _w in (("b_ada", 48), ("norm1_g", 8), ("norm2_g", 8), ("final_g", 8), ("b_fm", 56),
               ("cw", 120), ("cb", 40), ("skip", 16)):
    COLV[_n] = (_o, _w)
    _o += _w
NCOLV = _o
ROWV = {}
_o = 0
for _n, _w in (("bv", 1024), ("bo", 1024), ("bg", 16), ("mng", 1024)):
    ROWV[_n] = (_o, _w)
    _o += _w
NROWV = _o


def fm_src_col(cc):
    if cc < 8:
        return K_OFF + cc * 128
    if cc < 16:
        return Q_OFF + (cc - 8) * 128
    if cc < 40:
        return HY_OFF + (cc - 16) * 128
    return MG_OFF + (cc - 40) * 128


def build_program(NB=NB_FULL, debug=(), stop_after=99):
    nc = bass.Bass("TRN2", target_bir_lowering=False)
    P = Prog(nc)

    def din(name, shape, dt=F32):
        return nc.dram_tensor(name, list(shape), dt, kind="ExternalInput").ap()

    def dscr(name, shape, dt):
        kind = "ExternalOutput" if name in debug else "Internal"
        return nc.dram_tensor(name, list(shape), dt, kind=kind).ap()

    xT = din("xT", [NB, D, L])
    ctxT = din("ctxT", [NB, D, CL])
    cT = din("cT", [128, 8, 5])
    w_ada = din("w_ada", [D, 6 * D])
    w_in = din("w_in", [D, NIN])
    w_pm = din("w_pm", [D, D])
    w_ph = din("w_ph", [D, D])
    w_out = din("w_out", [D, D])
    ffn_w1 = din("ffn_w1", [D, DFF])
    ffn_w3 = din("ffn_w3", [D, DFF])
    ffn_w2 = din("ffn_w2", [DFF, D])
    hy_w1 = din("hy_w1", [33, 64])
    hy_w2 = din("hy_w2", [64, 64])
    hy_w3 = din("hy_w3", [64, 64])
    hy_w4 = din("hy_w4", [64, 4096])
    hyv_d = din("hyv", [64, 4])
    colv_d = din("colv", [128, NCOLV])
    rowv_d = din("rowv", [128, NROWV])
    FTs = din("FTs", [32, 128, 16, 128], BF16)
    Fms = din("Fms", [8, 128, 32, 256], BF16)
    zT_d = din("zT", [33, L])
    tvec_d = din("tvec", [128, 16])
    decay_d = din("decay_bc", [128, 4096])
    outT = nc.dram_tensor("outT", [NB, D, L], F32, kind="ExternalOutput").ap()

    Wfm = dscr("Wfm", [56, 128, 8, 128], BF16)
    Wtm = dscr("Wtm", [8, 128, 8, 256], BF16)
    Wg = dscr("Wg", [128, 8, 16], BF16)
    Wpm = dscr("Wpm", [128, 8, D], BF16)
    Wph = dscr("Wph", [128, 8, D], BF16)
    Wout = dscr("Wout", [128, 8, D], BF16)
    W1 = dscr("W1", [NFC, 128, 8, 128], BF16)
    W3 = dscr("W3", [NFC, 128, 8, 128], BF16)
    W2 = dscr("W2", [8, 128, NFC, 128], BF16)
    Kspec = dscr("Kspec", [2, 2, 16, 128, D], F32)
    fmT_s = dscr("fmT_s", [56, 128, L], BF16)
    kTc_s = dscr("kTc_s", [8, 128, CL], BF16)
    v_s = dscr("v_s", [4, 128, 16, 256], BF16)
    o_s = dscr("o_s", [4, 128, 16, 256], BF16)
    vc_s = dscr("vc_s", [4, 128, 2, 256], BF16)
    g_s = dscr("g_s", [128, 18, 16], F32)
    hmT_s = dscr("hmT_s", [8, 128, L], BF16)
    hhT_s = dscr("hhT_s", [8, 128, L], BF16)

    st = ExitStack()
    ARENA_BF = 106000
    arena = st.enter_context(nc.sbuf_tensor("arena", [128, ARENA_BF], BF16))
    psum = [(st.enter_context(nc.psum_tensor("ps%d" % i, [128, 512], F32)), Buf("ps")) for i in range(8)]
    psi = [0]

    def PS():
        it = psum[psi[0] % 8]
        psi[0] += 1
        return it

    aoff = [0]

    def view(dt, shape, np_=128):
        n = int(np.prod(shape[1:]))
        nb = n * 2 if dt == F32 else n
        nb = (nb + 15) // 16 * 16
        assert aoff[0] + nb <= ARENA_BF, ("SBUF overflow", aoff[0], nb)
        a = arena[0:shape[0], aoff[0]:aoff[0] + nb]
        aoff[0] += nb
        if dt == F32:
            a = a.bitcast(F32)
        a = a[:, 0:n]
        if len(shape) == 3:
            a = a.rearrange("p (a b) -> p a b", a=shape[1])
        elif len(shape) == 4:
            a = a.rearrange("p (a b c) -> p a b c", a=shape[1], b=shape[2])
        return a

    def tile(dt, shape):
        return view(dt, shape), Buf("t")

    def ring(n, dt, shape):
        return Ring([view(dt, shape) for _ in range(n)])

    def act(out, in_, func, reads, writes, bias=None, scale=None):
        kw = {}
        if bias is not None:
            kw["bias"] = bias
        if scale is not None:
            kw["scale"] = scale
        return P.op("act", lambda e: e.activation(out=out, in_=in_, func=func, **kw), reads, writes)

    def tt(eng, out, in0, in1, op, reads, writes):
        return P.op(eng, lambda e: e.tensor_tensor(out=out, in0=in0, in1=in1, op=op), reads, writes)

    def ts(eng, out, in0, s1, op0, reads, writes, s2=None, op1=None):
        if op1 is None:
            return P.op(eng, lambda e: e.tensor_scalar(out=out, in0=in0, scalar1=s1, scalar2=None, op0=op0), reads, writes)
        return P.op(eng, lambda e: e.tensor_scalar(out=out, in0=in0, scalar1=s1, scalar2=s2, op0=op0, op1=op1), reads, writes)

    def stt(eng, out, in0, scalar, in1, op0, op1, reads, writes):
        return P.op(eng, lambda e: e.scalar_tensor_tensor(out=out, in0=in0, scalar=scalar, in1=in1, op0=op0, op1=op1), reads, writes)

    def cp(eng, out, in_, reads, writes):
        if eng == "act":
            return act(out, in_, AF.Copy, reads, writes)
        return P.op(eng, lambda e: e.tensor_copy(out=out, in_=in_), reads, writes)

    def mm(out, lhsT, rhs, start, stop, reads, writes):
        return P.mm(lambda e: e.matmul(out, lhsT=lhsT, rhs=rhs, start=start, stop=stop), reads, writes)

    def tr(out, in_, reads, writes):
        return P.mm(lambda e: e.transpose(out=out, in_=in_, identity=ident), reads, writes)

    def ld(out, in_, owner, reads=(), q="sp"):
        return P.dma(q, lambda e: e.dma_start(out=out, in_=in_), owner, reads=reads, writes=[owner])

    def stq(out, in_, owner, reads, q="pool"):
        return P.dma(q, lambda e: e.dma_start(out=out, in_=in_), owner, reads=reads)

    colv, Bcolv = tile(F32, [128, NCOLV])
    rowv, Browv = tile(F32, [128, NROWV])
    ident, Bident = tile(BF16, [128, 128])
    maskF, BmaskF = tile(F32, [128, 128])
    maskB, BmaskB = tile(F32, [128, 128])
    onesf, Bones = tile(F32, [128, 128])
    epsT, Beps = tile(F32, [128, 1])
    modT, Bmod = tile(F32, [128, 48, 5])
    s1T, Bs1 = tile(F32, [128, 8, 5])
    s2T, Bs2 = tile(F32, [128, 8, 5])
    knyq, Bknyq = tile(F32, [1, 2, D])
    tvec, Btvec = tile(F32, [128, 16])
    w1b, Bw1b = tile(F32, [128, 40])
    CONST = [Bcolv, Browv, Bident, BmaskF, BmaskB, Bones, Beps, Bmod, Bs1, Bs2, Bknyq, Btvec]
    persist_mark = aoff[0]

    def cv(name, i=0, n=1):
        o, w = COLV[name]
        return colv[:, o + i:o + i + n]

    def rv(name, i=0, n=None):
        o, w = ROWV[name]
        n = w if n is None else n
        return rowv[:, o + i:o + i + n]

    ld(colv, colv_d[:, :], Bcolv)
    ld(rowv, rowv_d[:, :], Browv)
    ld(tvec, tvec_d[:, :], Btvec)
    P.op("pool", lambda e: e.memset(onesf, 1.0), (), [Bones])
    P.op("pool", lambda e: e.memset(epsT, EPS), (), [Beps])
    P.op("pool", lambda e: e.memset(maskF, 1.0), (), [BmaskF])
    P.op("pool", lambda e: e.memset(maskB, 1.0), (), [BmaskB])
    P.op("pool", lambda e: e.affine_select(out=maskF, in_=maskF, pattern=[[1, 128]], compare_op=ALU.is_ge, fill=0.0, base=0, channel_multiplier=-1), [BmaskF], [BmaskF])
    P.op("pool", lambda e: e.affine_select(out=maskB, in_=maskB, pattern=[[-1, 128]], compare_op=ALU.is_ge, fill=0.0, base=0, channel_multiplier=1), [BmaskB], [BmaskB])
    idf, Bidf = tile(F32, [128, 128])
    P.op("pool", lambda e: e.memset(idf, 1.0), (), [Bidf])
    P.op("pool", lambda e: e.affine_select(out=idf, in_=idf, pattern=[[-1, 128]], compare_op=ALU.is_equal, fill=0.0, base=0, channel_multiplier=1), [Bidf], [Bidf])
    cp("dve", ident, idf, [Bidf], [Bident])
    _cwv = colv[:, COLV["cw"][0]:COLV["cw"][0] + 120].rearrange("p (c j) -> p c j", j=3)
    tt("dve", w1b, _cwv[:, :, 1], cv("b_fm", 0, 40), ALU.mult, [Bcolv], [Bw1b])

    cast_owner = [Buf("cast%d" % i) for i in range(8)]
    cidx = [0]
    pending_casts = []

    def cast(out, in_):
        pending_casts.append((out, in_))

    def flush_casts(n):
        for _ in range(min(n, len(pending_casts))):
            out, in_ = pending_casts.pop(0)
            b = cast_owner[cidx[0] % len(cast_owner)]
            cidx[0] += 1
            P.dma("pool", lambda e, out=out, in_=in_: e.dma_start(out=out, in_=in_), b, deferred=True)

    def kview(w2d):
        return w2d.rearrange("(kc p) c -> p kc c", p=128)

    for cc in range(56):
        c0 = fm_src_col(cc)
        cast(Wfm[cc], kview(w_in[:, c0:c0 + 128]))
    for h in range(4):
        cast(Wtm[h], kview(w_in[:, V_OFF + h * 256:V_OFF + (h + 1) * 256]))
        cast(Wtm[4 + h], kview(w_in[:, O_OFF + h * 256:O_OFF + (h + 1) * 256]))
    cast(Wg[:, :, :], kview(w_in[:, G_OFF:G_OFF + 16]))
    for kc in range(8):
        cast(Wpm[:, kc, :], w_pm[kc * 128:(kc + 1) * 128, :])
        cast(Wph[:, kc, :], w_ph[kc * 128:(kc + 1) * 128, :])
        cast(Wout[:, kc, :], w_out[kc * 128:(kc + 1) * 128, :])
    for fc in range(NFC):
        cast(W1[fc], kview(ffn_w1[:, fc * 128:(fc + 1) * 128]))
        cast(W3[fc], kview(ffn_w3[:, fc * 128:(fc + 1) * 128]))
    for cc in range(8):
        cast(W2[cc], ffn_w2[:, cc * 128:(cc + 1) * 128].rearrange("(fc p) c -> p fc c", p=128))

    flush_casts(16)

    m0 = aoff[0]
    sc, Bsc = tile(F32, [128, 8, 5])
    ld(sc, cT[:, :, :], Bsc)
    act(sc, sc, AF.Silu, [Bsc], [Bsc])
    wr = ring(3, F32, [128, 8, 128])
    for oc in range(48):
        wt, Bw = wr.get()
        ld(wt, kview(w_ada[:, oc * 128:(oc + 1) * 128]), Bw)
        ps, Bp = PS()
        for kc in range(8):
            mm(ps[:, 0:5], wt[:, kc, :], sc[:, kc, :], kc == 0, kc == 7, [Bw, Bsc], [Bp])
        act(modT[:, oc, :], ps[:, 0:5], AF.Identity, [Bp, Bcolv], [Bmod], bias=cv("b_ada", oc))
    ts("dve", s1T, modT[:, 8:16, :], 1.0, ALU.add, [Bmod], [Bs1])
    tt("dve", s1T, s1T, cv("norm1_g", 0, 8).unsqueeze(2).to_broadcast([128, 8, 5]), ALU.mult, [Bs1, Bcolv], [Bs1])
    ts("dve", s2T, modT[:, 32:40, :], 1.0, ALU.add, [Bmod], [Bs2])
    tt("dve", s2T, s2T, cv("norm2_g", 0, 8).unsqueeze(2).to_broadcast([128, 8, 5]), ALU.mult, [Bs2, Bcolv], [Bs2])
    P.barrier()
    aoff[0] = m0

    if stop_after >= 1:
        m0 = aoff[0]
        aA, BaA = tile(F32, [64, L])
        w4t, Bw4 = tile(F32, [64, 4096])
        ld(w4t, hy_w4[:, :], Bw4)
        m1 = aoff[0]
        zt, Bzt = tile(F32, [33, L])
        w1t, Bw1 = tile(F32, [33, 64])
        w2t, Bw2 = tile(F32, [64, 64])
        w3t, Bw3 = tile(F32, [64, 64])
        hyv, Bhyv = tile(F32, [64, 4])
        fb, Bfb = tile(F32, [64, 3])
        ld(zt, zT_d[:, :], Bzt)
        ld(w1t, hy_w1[:, :], Bw1)
        ld(w2t, hy_w2[:, :], Bw2)
        ld(w3t, hy_w3[:, :], Bw3)
        ld(hyv, hyv_d[:, :], Bhyv)
        tt("dve", fb, hyv[:, 0:3], hyv[:, 3:4].to_broadcast([64, 3]), ALU.mult, [Bhyv], [Bfb])
        aB, BaB = tile(F32, [64, L])
        w1s, Bw1s = tile(F32, [64, L])
        w2s, Bw2s = tile(F32, [64, L])
        prev, Bprev = zt, Bzt
        for li, (wt_, Bwt_) in enumerate(((w1t, Bw1), (w2t, Bw2), (w3t, Bw3))):
            cur, Bcur = (aA, BaA) if li % 2 == 0 else (aB, BaB)
            for t4 in range(4):
                ps, Bp = PS()
                mm(ps[0:64, :], wt_, prev[:, t4 * 512:(t4 + 1) * 512], True, True, [Bwt_, Bprev], [Bp])
                act(cur[:, t4 * 512:(t4 + 1) * 512], ps[0:64, :], AF.Identity, [Bp, Bhyv, Bfb], [Bcur],
                    bias=fb[:, li:li + 1], scale=hyv[:, 3:4])
            ts("dve", w1s, cur, -math.pi, ALU.is_lt, [Bcur], [Bw1s], s2=2 * math.pi, op1=ALU.mult)
            ts("dve", w2s, cur, math.pi, ALU.is_gt, [Bcur], [Bw2s], s2=-2 * math.pi, op1=ALU.mult)
            tt("dve", cur, cur, w1s, ALU.add, [Bcur, Bw1s], [Bcur])
            tt("dve", cur, cur, w2s, ALU.add, [Bcur, Bw2s], [Bcur])
            act(cur, cur, AF.Sin, [Bcur], [Bcur])
            prev, Bprev = cur, Bcur
        a3, Ba3 = prev, Bprev
        assert a3 is aA
        P.barrier()
        aoff[0] = m1
        absd, Babsd = tile(F32, [128, 4096])
        ld(absd, decay_d[:, :], Babsd)
        act(absd, absd, AF.Abs, [Babsd], [Babsd])
        ntv, Bntv = tile(F32, [128, 16])
        ts("dve", ntv, tvec, -1.0, ALU.mult, [Btvec], [Bntv])
        Eall, BE = tile(BF16, [128, 16, 1024])
        Oall, BO = tile(BF16, [128, 16, 1024])
        winr = ring(2, F32, [128, 512])
        tapr = ring(2, F32, [128, 2048])
        ftr = ring(4, BF16, [128, 16, 128])
        kor = ring(4, F32, [128, 512])
        for o in range(2):
            for tc in range(16):
                flush_casts(2)
                taps, Btap = tapr.get()
                for cb in range(4):
                    gsl = slice(o * 2048 + cb * 512, o * 2048 + (cb + 1) * 512)
                    ps, Bp = PS()
                    mm(ps[:, :], a3[:, tc * 128:(tc + 1) * 128], w4t[:, gsl], True, True, [Ba3, Bw4], [Bp])
                    win, Bwin = winr.get()
                    act(win, absd[:, gsl], AF.Exp, [Babsd, Bntv], [Bwin], scale=ntv[:, tc:tc + 1])
                    tt("dve", taps[:, cb * 512:(cb + 1) * 512], ps[:, :], win, ALU.mult, [Bp, Bwin], [Btap])
                tt("pool", Eall[:, tc, :], taps[:, 0:1024], taps[:, 1024:2048], ALU.add, [Btap], [BE])
                tt("pool", Oall[:, tc, :], taps[:, 0:1024], taps[:, 1024:2048], ALU.subtract, [Btap], [BO])
            for fi in range(16):
                flush_casts(2)
                fre, Bfre = ftr.get()
                ld(fre, FTs[fi], Bfre)
                fim, Bfim = ftr.get()
                ld(fim, FTs[16 + fi], Bfim)
                for ch in range(2):
                    csl = slice(ch * 512, ch * 512 + 512)
                    psR, BpR = PS()
                    for jc in range(16):
                        mm(psR[:, :], fre[:, jc, :], Eall[:, jc, csl], jc == 0, jc == 15, [Bfre, BE], [BpR])
                    psI, BpI = PS()
                    for jc in range(16):
                        mm(psI[:, :], fim[:, jc, :], Oall[:, jc, csl], jc == 0, jc == 15, [Bfim, BO], [BpI])
                    kr, Bkr = kor.get()
                    act(kr, psR[:, :], AF.Copy, [BpR], [Bkr], scale=2.0 / NFFT)
                    ki, Bki = kor.get()
                    act(ki, psI[:, :], AF.Copy, [BpI], [Bki], scale=2.0 / NFFT)
                    if fi == 0:
                        ts("dve", kr[0:1, :], kr[0:1, :], 0.5, ALU.mult, [Bkr], [Bkr])
                        P.op("dve", lambda e, ki=ki: e.memset(ki[0:1, :], 0.0), [Bki], [Bki])
                        psN, BpN = PS()
                        for jc in range(16):
                            mm(psN[:, :], fim[:, jc, :], Eall[:, jc, csl], jc == 0, jc == 15, [Bfim, BE], [BpN])
                        act(knyq[0:1, o, ch * 512:(ch + 1) * 512], psN[0:1, :], AF.Copy, [BpN], [Bknyq], scale=1.0 / NFFT)
                    stq(Kspec[o, 0, fi, :, ch * 512:(ch + 1) * 512], kr, Bkr, [Bkr])
                    stq(Kspec[o, 1, fi, :, ch * 512:(ch + 1) * 512], ki, Bki, [Bki])
        P.barrier()
        aoff[0] = m0

    def stage1(src, T, j, is_ctx):
        m0 = aoff[0]
        TT = min(512, T)
        ntt = T // TT
        hT = view(BF16, [128, 8, T])
        BhTs = [Buf("hT") for _ in range(ntt)]
        xr = ring(2, F32, [128, 8, TT])
        sqr = ring(1, F32, [128, 8, TT])
        rsr = ring(2, F32, [128, TT])
        tmr = ring(2, F32, [128, TT])
        for t_ in range(ntt):
            tsl = slice(t_ * TT, (t_ + 1) * TT)
            xt, Bx = xr.get()
            ld(xt, src[:, tsl].rearrange("(kc p) t -> p kc t", p=128), Bx)
            sq, Bsq = sqr.get()
            act(sq, xt, AF.Square, [Bx], [Bsq])
            ps, Bp = PS()
            for kc in range(8):
                mm(ps[:, 0:TT], onesf, sq[:, kc, :], kc == 0, kc == 7, [Bones, Bsq], [Bp])
            rs, Brs = rsr.get()
            act(rs, ps[:, 0:TT], AF.Sqrt, [Bp, Beps], [Brs], bias=epsT[:, 0:1], scale=1.0 / D)
            P.op("dve", lambda e, rs=rs: e.reciprocal(out=rs, in_=rs), [Brs], [Brs])
            for kc in range(8):
                tm, Btm = tmr.get()
                stt("dve", tm, xt[:, kc, :], s1T[:, kc, j:j + 1], rs, ALU.mult, ALU.mult, [Bx, Bs1, Brs], [Btm])
                act(hT[:, kc, tsl], tm, AF.Identity, [Btm, Bmod], [BhTs[t_]], bias=modT[:, kc, j:j + 1])
        wr_ = ring(3, BF16, [128, 8, 128])
        pbr = ring(3, F32, [128, TT])
        yr = ring(3, F32, [128, TT])
        osr = ring(3, BF16, [128, TT])
        RW = 64 if not is_ctx else T
        NR = TT // RW
        cclist = range(8) if is_ctx else range(56)
        pending = []

        def finish():
            while pending:
                pending.pop(0)()

        for cc in cclist:
            wt, Bw = wr_.get()
            ld(wt, Wfm[cc], Bw)
            for t_ in range(ntt):
                tsl = slice(t_ * TT, (t_ + 1) * TT)
                ps, Bp = PS()
                for kc in range(8):
                    mm(ps[:, 0:TT], wt[:, kc, :], hT[:, kc, tsl], kc == 0, kc == 7, [Bw, BhTs[t_]], [Bp])
                ost, Bos = osr.get()
                dst = kTc_s[cc, :, tsl] if is_ctx else fmT_s[cc, :, tsl]
                if cc < 40:
                    pb, Bpb = pbr.get()
                    act(pb, ps[:, 0:TT], AF.Identity, [Bp, Bcolv], [Bpb], bias=cv("b_fm", cc))
                    y, By = yr.get()
                    cwo = COLV["cw"][0] + cc * 3
                    act(y, ps[:, 0:TT], AF.Identity, [Bp, Bcolv, Bw1b], [By], bias=w1b[:, cc:cc + 1], scale=colv[:, cwo + 1:cwo + 2])
                    finish()
                    y3 = y.rearrange("p (r w) -> p r w", w=RW)
                    p3 = pb.rearrange("p (r w) -> p r w", w=RW)
                    stt("dve", y3[:, :, 1:RW], p3[:, :, 0:RW - 1], colv[:, cwo:cwo + 1], y3[:, :, 1:RW], ALU.mult, ALU.add, [Bpb, By, Bcolv], [By])
                    stt("dve", y3[:, :, 0:RW - 1], p3[:, :, 1:RW], colv[:, cwo + 2:cwo + 3], y3[:, :, 0:RW - 1], ALU.mult, ALU.add, [Bpb, By, Bcolv], [By])

                    def fin(ost=ost, Bos=Bos, y=y, By=By, cc=cc, dst=dst):
                        act(ost, y, AF.Silu if cc < 16 else AF.Identity, [By, Bcolv], [Bos], bias=cv("cb", cc))
                        stq(dst, ost, Bos, [Bos])
                    pending.append(fin)
                else:
                    finish()
                    act(ost, ps[:, 0:TT], AF.Sigmoid, [Bp, Bcolv], [Bos], bias=cv("b_fm", cc))
                    stq(dst, ost, Bos, [Bos])
        finish()
        wtm, Bwtm = tile(BF16, [128, 8, 8, 256])
        for i in range(8):
            if is_ctx and i >= 4:
                continue
            P.dma("sp", lambda e, i=i: e.dma_start(out=wtm[:, i, :, :], in_=Wtm[i]), Bwtm, writes=[Bwtm])
        wg, Bwg = tile(BF16, [128, 8, 16])
        ld(wg, Wg[:, :, :], Bwg)
        ntc = T // 128
        gt, Bgt = tile(F32, [128, ntc, 16])
        vsr = ring(3, BF16, [128, 256])
        tmo = ring(2, F32, [128, 256])
        for tc in range(ntc):
            csl = slice(tc * 128, (tc + 1) * 128)
            for i in range(8):
                if is_ctx and i >= 4:
                    continue
                ps, Bp = PS()
                for kc in range(8):
                    mm(ps[:, 0:256], hT[:, kc, csl], wtm[:, i, kc, :], kc == 0, kc == 7, [BhTs[(tc * 128) // TT], Bwtm], [Bp])
                vs, Bvs = vsr.get()
                if i < 4:
                    tt("dve", vs, ps[:, 0:256], rv("bv", i * 256, 256), ALU.add, [Bp, Browv], [Bvs])
                    dst = vc_s[i, :, tc, :] if is_ctx else v_s[i, :, tc, :]
                else:
                    tm, Btm = tmo.get()
                    tt("dve", tm, ps[:, 0:256], rv("bo", (i - 4) * 256, 256), ALU.add, [Bp, Browv], [Btm])
                    act(vs, tm, AF.Sigmoid, [Btm], [Bvs])
                    dst = o_s[i - 4, :, tc, :]
                stq(dst, vs, Bvs, [Bvs])
            ps, Bp = PS()
            for kc in range(8):
                mm(ps[:, 0:16], hT[:, kc, csl], wg[:, kc, :], kc == 0, kc == 7, [BhTs[(tc * 128) // TT], Bwg], [Bp])
            tt("dve", gt[:, tc, :], ps[:, 0:16], rv("bg"), ALU.add, [Bp, Browv], [Bgt])
        if is_ctx:
            stq(g_s[:, 16:18, :], gt, Bgt, [Bgt])
        else:
            stq(g_s[:, 0:16, :], gt, Bgt, [Bgt])
        P.barrier()
        aoff[0] = m0

    def stage2():
        m0 = aoff[0]
        G, BG = tile(F32, [128, 18, 16])
        ld(G, g_s[:, :, :], BG)
        ax, Bax = tile(F32, [128, 18, 16])
        LS, BLS = tile(F32, [128, 18, 16])
        act(ax, G, AF.Abs, [BG], [Bax])
        act(ax, ax, AF.Exp, [Bax], [Bax], scale=-1.0)
        act(ax, ax, AF.Ln, [Bax], [Bax], bias=1.0)
        ts("dve", LS, G, 0.0, ALU.min, [BG], [BLS])
        tt("dve", LS, LS, ax, ALU.subtract, [BLS, Bax], [BLS])
        LS2 = LS.rearrange("p a b -> p (a b)")
        cumF, BcF = tile(F32, [128, 18, 16])
        cumB, BcB = tile(F32, [128, 18, 16])
        tot, Btot = tile(F32, [128, 18, 16])
        for (lh, Blh, dstt, Bd) in ((maskF, BmaskF, cumF, BcF), (maskB, BmaskB, cumB, BcB), (onesf, Bones, tot, Btot)):
            ps, Bp = PS()
            mm(ps[:, 0:288], lh, LS2, True, True, [Blh, BLS], [Bp])
            cp("act", dstt.rearrange("p a b -> p (a b)"), ps[:, 0:288], [Bp], [Bd])
        gate = {}
        for dname, cum, Bc, lcol, fcol in (("f", cumF, BcF, 0, 4), ("b", cumB, BcB, 8, 12)):
            A_, BA_ = tile(F32, [128, 18, 4])
            BK_, BBK_ = tile(F32, [128, 18, 4])
            DEC_, BDEC_ = tile(F32, [128, 18, 4])
            act(A_, cum[:, :, fcol:fcol + 4], AF.Exp, [Bc], [BA_])
            tt("dve", BK_, G[:, :, lcol:lcol + 4], cum[:, :, fcol:fcol + 4], ALU.subtract, [BG, Bc], [BBK_])
            act(BK_, BK_, AF.Exp, [BBK_], [BBK_])
            act(DEC_, tot[:, :, fcol:fcol + 4], AF.Exp, [Btot], [BDEC_])
            gate[dname] = (A_, BA_, BK_, BBK_, DEC_, BDEC_)
        qT, BqT = tile(BF16, [128, 2, L])
        kT, BkT = tile(BF16, [128, 2, L + CL])
        v, Bv = tile(BF16, [128, 18, 256])
        ot, Bot = tile(BF16, [128, 16, 256])
        ktok, Bktok = tile(BF16, [128, 18, 256])
        vt = {"f": tile(BF16, [128, 18, 264]), "b": tile(BF16, [128, 18, 264])}
        SS = {"f": tile(BF16, [128, 16, 128]), "b": tile(BF16, [128, 16, 128])}
        raw = {"f": tile(F32, [128, 16, 264]), "b": tile(F32, [128, 16, 264])}
        Tst = {"f": [tile(F32, [128, 2, 264]), tile(F32, [128, 2, 264])], "b": [tile(F32, [128, 2, 264]), tile(F32, [128, 2, 264])]}
        Cr = {"f": ring(2, BF16, [128, 2, 264]), "b": ring(2, BF16, [128, 2, 264])}
        hsum, Bhs = tile(F32, [128, 16, 256])
        hmtok, Bhmt = tile(BF16, [128, 16, 256])
        hmT, BhmT = tile(BF16, [128, 2, L])
        sm = {k_: tile(F32, [128, 16]) for k_ in ("den", "c_f", "c_b", "sum", "ssq", "mean", "rstd", "t0", "nmr")}
        sfr = ring(3, F32, [128, 128])
        def pre(h):
            for dc in range(2):
                P.dma("sp", lambda e, dc=dc, h=h: e.dma_start(out=qT[:, dc, :], in_=fmT_s[8 + 2 * h + dc]), BqT, writes=[BqT])
                P.dma("sp", lambda e, dc=dc, h=h: e.dma_start(out=kT[:, dc, 0:L], in_=fmT_s[2 * h + dc]), BkT, writes=[BkT])
                P.dma("sp", lambda e, dc=dc, h=h: e.dma_start(out=kT[:, dc, L:L + CL], in_=kTc_s[2 * h + dc]), BkT, writes=[BkT])
            P.dma("sp", lambda e, h=h: e.dma_start(out=v[:, 0:16, :], in_=v_s[h]), Bv, writes=[Bv])
            P.dma("sp", lambda e, h=h: e.dma_start(out=v[:, 16:18, :], in_=vc_s[h]), Bv, writes=[Bv])
            for dname in ("f", "b"):
                A_, BA_, BK_, BBK_, DEC_, BDEC_ = gate[dname]
                vtt, Bvt = vt[dname]
                veng = "dve" if dname == "f" else "pool"
                tt(veng, vtt[:, :, 0:256], v, BK_[:, :, h:h + 1].to_broadcast([128, 18, 256]), ALU.mult, [Bv, BBK_], [Bvt])
                cp(veng, vtt[:, :, 256:257], BK_[:, :, h:h + 1], [BBK_], [Bvt])
            for ch in range(18):
                ps, Bp = PS()
                pb = ps[:, 0:128].bitcast(BF16)
                for dc in range(2):
                    tr(pb[:, dc * 128:(dc + 1) * 128], kT[:, dc, ch * 128:(ch + 1) * 128], [BkT, Bident], [Bp])
                cp("act" if ch % 2 else "dve", ktok[:, ch, :], pb, [Bp], [Bktok])
            for ch in range(16):
                csl = slice(ch * 128, (ch + 1) * 128)
                ps, Bp = PS()
                for dc in range(2):
                    mm(ps[:, 0:128], kT[:, dc, csl], qT[:, dc, csl], dc == 0, dc == 1, [BkT, BqT], [Bp])
                sfu, Bsfu = sfr.get()
                cp("act", sfu, ps[:, 0:128], [Bp], [Bsfu])
                P.op("pool", lambda e, o_=SS["f"][0][:, ch, :], i_=sfu: e.affine_select(out=o_, in_=i_, pattern=[[1, 128]], compare_op=ALU.is_ge, fill=0.0, base=0, channel_multiplier=-1), [Bsfu], [SS["f"][1]])
                P.op("pool", lambda e, o_=SS["b"][0][:, ch, :], i_=sfu: e.affine_select(out=o_, in_=i_, pattern=[[-1, 128]], compare_op=ALU.is_ge, fill=0.0, base=0, channel_multiplier=1), [Bsfu], [SS["b"][1]])
        def chain(h):
            ld(ot, o_s[h], Bot)
            tt("pool", ot, ot, rv("mng", h * 256, 256).unsqueeze(1).to_broadcast([128, 16, 256]), ALU.mult, [Bot, Browv], [Bot])
            order = {"f": [16, 17] + list(range(16)), "b": [17, 16] + list(range(15, -1, -1))}
            cur = {"f": 0, "b": 0}
            upend = {}

            def emit_U(i):
                for dname in ("f", "b"):
                    vtt, Bvt = vt[dname]
                    ch = order[dname][i]
                    lst = []
                    for kc in range(2):
                        psu, Bpu = PS()
                        mm(psu[:, 0:257], ktok[:, ch, kc * 128:(kc + 1) * 128], vtt[:, ch, 0:257], True, True, [Bktok, Bvt], [Bpu])
                        lst.append((psu, Bpu))
                    upend[(dname, i)] = lst

            emit_U(0)
            rawpend = []
            for i in range(18):
                for dname in ("f", "b"):
                    A_, BA_, BK_, BBK_, DEC_, BDEC_ = gate[dname]
                    Tc, BTc = Tst[dname][cur[dname]]
                    Tn, BTn = Tst[dname][1 - cur[dname]]
                    prev = order[dname][i - 1] if i > 0 else None
                    if i < 17:
                        for kc in range(2):
                            psu, Bpu = upend[(dname, i)][kc]
                            if i == 0:
                                cp("dve", Tn[:, kc, 0:257], psu[:, 0:257], [Bpu], [BTn])
                            else:
                                stt("dve", Tn[:, kc, 0:257], Tc[:, kc, 0:257], DEC_[:, prev, h:h + 1], psu[:, 0:257], ALU.mult, ALU.add, [BTc, BDEC_, Bpu], [BTn])
                if i + 1 < 17:
                    emit_U(i + 1)
                newpend = []
                for dname in ("f", "b"):
                    A_, BA_, BK_, BBK_, DEC_, BDEC_ = gate[dname]
                    vtt, Bvt = vt[dname]
                    Tc, BTc = Tst[dname][cur[dname]]
                    ch = order[dname][i]
                    prev = order[dname][i - 1] if i > 0 else None
                    if i >= 2:
                        Cb, BCb = Cr[dname].get()
                        act(Cb, Tc, AF.Copy, [BTc, BDEC_], [BCb], scale=DEC_[:, prev, h:h + 1])
                        pso, Bpo = PS()
                        csl = slice(ch * 128, (ch + 1) * 128)
                        mm(pso[:, 0:257], SS[dname][0][:, ch, :], vtt[:, ch, 0:257], True, False, [SS[dname][1], Bvt], [Bpo])
                        for kc in range(2):
                            mm(pso[:, 0:257], qT[:, kc, csl], Cb[:, kc, 0:257], False, kc == 1, [BqT, BCb], [Bpo])
                        newpend.append((dname, ch, pso, Bpo))
                    cur[dname] = 1 - cur[dname]
                for dname, ch_, pso_, Bpo_ in rawpend:
                    cp("act", raw[dname][0][:, ch_, 0:257], pso_[:, 0:257], [Bpo_], [raw[dname][1]])
                rawpend = newpend
            for dname, ch_, pso_, Bpo_ in rawpend:
                cp("act", raw[dname][0][:, ch_, 0:257], pso_[:, 0:257], [Bpo_], [raw[dname][1]])
        def post(h):
            for dname in ("f", "b"):
                A_, BA_ = gate[dname][0], gate[dname][1]
                rw, Brw = raw[dname]
                den, Bden = sm["den"]
                cc_, Bcc = sm["c_" + dname]
                t0, Bt0 = sm["t0"]
                tt("dve", den, rw[:, :, 256], A_[:, 0:16, h], ALU.mult, [Brw, BA_], [Bden])
                stt("dve", t0, den, -1.0, den, ALU.mult, ALU.max, [Bden], [Bt0])
                ts("dve", t0, t0, 16.0, ALU.max, [Bt0], [Bt0])
                P.op("dve", lambda e, t0=t0: e.reciprocal(out=t0, in_=t0), [Bt0], [Bt0])
                tt("dve", cc_, t0, A_[:, 0:16, h], ALU.mult, [Bt0, BA_], [Bcc])
            tt("dve", hsum, raw["f"][0][:, :, 0:256], sm["c_f"][0].unsqueeze(2).to_broadcast([128, 16, 256]), ALU.mult, [raw["f"][1], sm["c_f"][1]], [Bhs])
            rb_ = raw["b"][0][:, :, 0:256]
            tt("pool", rb_, rb_, sm["c_b"][0].unsqueeze(2).to_broadcast([128, 16, 256]), ALU.mult, [raw["b"][1], sm["c_b"][1]], [raw["b"][1]])
            tt("dve", hsum, hsum, rb_, ALU.add, [Bhs, raw["b"][1]], [Bhs])
            tmpb, Btmpb = raw["f"][0][:, :, 0:256], raw["f"][1]
            sum_, Bsum = sm["sum"]
            ssq, Bssq = sm["ssq"]
            mean, Bmean = sm["mean"]
            rstd, Brstd = sm["rstd"]
            P.op("dve", lambda e: e.tensor_reduce(out=sum_, in_=hsum, axis=AX.X, op=ALU.add), [Bhs], [Bsum])
            act(tmpb, hsum, AF.Square, [Bhs], [Btmpb])
            P.op("dve", lambda e: e.tensor_reduce(out=ssq, in_=tmpb, axis=AX.X, op=ALU.add), [Btmpb], [Bssq])
            ts("dve", mean, sum_, 1.0 / DH, ALU.mult, [Bsum], [Bmean])
            t0, Bt0 = sm["t0"]
            tt("dve", t0, mean, mean, ALU.mult, [Bmean], [Bt0])
            stt("dve", rstd, ssq, 1.0 / DH, t0, ALU.mult, ALU.subtract, [Bssq, Bt0], [Brstd])
            act(rstd, rstd, AF.Sqrt, [Brstd, Beps], [Brstd], bias=epsT[:, 0:1])
            P.op("dve", lambda e: e.reciprocal(out=rstd, in_=rstd), [Brstd], [Brstd])
            nmr, Bnmr = sm["nmr"]
            stt("dve", nmr, mean, -1.0, rstd, ALU.mult, ALU.mult, [Bmean, Brstd], [Bnmr])
            for tc in range(16):
                act(hsum[:, tc, :], hsum[:, tc, :], AF.Identity, [Bhs, Brstd, Bnmr], [Bhs], bias=nmr[:, tc:tc + 1], scale=rstd[:, tc:tc + 1])
            tt("dve", hmtok, hsum, ot, ALU.mult, [Bhs, Bot], [Bhmt])
            for dc in range(2):
                for t4 in range(4):
                    ps, Bp = PS()
                    pb = ps[:, 0:256].bitcast(BF16)
                    for q_ in range(4):
                        tc = t4 * 4 + q_
                        tr(pb[:, q_ * 128:(q_ + 1) * 128], hmtok[:, tc, dc * 128:(dc + 1) * 128], [Bhmt, Bident], [Bp])
                    cp("act", hmT[:, dc, t4 * 512:(t4 + 1) * 512], pb, [Bp], [BhmT])
            for dc in range(2):
                P.dma("pool", lambda e, dc=dc, h=h: e.dma_start(out=hmT_s[2 * h + dc], in_=hmT[:, dc, :]), BhmT, reads=[BhmT])
        pre(0)
        for h in range(NH):
            chain(h)
            if h + 1 < NH:
                pre(h + 1)
            post(h)
        P.barrier()
        aoff[0] = m0

    def stage3():
        m0 = aoff[0]
        ftr = ring(4, BF16, [128, 16, 128])
        kr_r = ring(2, F32, [128, 512])
        ki_r = ring(2, F32, [128, 512])
        fmr = ring(2, BF16, [128, 32, 256])
        t_r = ring(8, F32, [128, 512])
        e_r = ring(3, F32, [128, 256])
        zT = [tile(BF16, [128, L]) for _ in range(4)]
        x1T, Bx1 = tile(BF16, [128, 4, L])
        x2T, Bx2 = tile(BF16, [128, 4, L])
        hh, Bhh = x2T, Bx2
        ztok, Bztok = tile(BF16, [128, 16, 512])
        Y, BY = tile(BF16, [128, 32, 512])
        for c2 in range(2):
            for i in range(4):
                ld(zT[i][0], fmT_s[16 + c2 * 4 + i], zT[i][1])
                P.dma("sp", lambda e, i=i, c2=c2: e.dma_start(out=x1T[:, i, :], in_=fmT_s[24 + c2 * 4 + i]), Bx1, writes=[Bx1])
                P.dma("sp", lambda e, i=i, c2=c2: e.dma_start(out=x2T[:, i, :], in_=fmT_s[32 + c2 * 4 + i]), Bx2, writes=[Bx2])
            for o in range(2):
                for tc in range(16):
                    ps, Bp = PS()
                    pb = ps[:, 0:256].bitcast(BF16)
                    for i in range(4):
                        tr(pb[:, i * 128:(i + 1) * 128], zT[i][0][:, tc * 128:(tc + 1) * 128], [zT[i][1], Bident], [Bp])
                    cp("act" if tc % 2 else "dve", ztok[:, tc, :], pb, [Bp], [Bztok])
                for fi in range(16):
                    fre, Bfre = ftr.get()
                    ld(fre, FTs[fi], Bfre)
                    fim, Bfim = ftr.get()
                    ld(fim, FTs[16 + fi], Bfim)
                    kr, Bkr = kr_r.get()
                    ld(kr, Kspec[o, 0, fi, :, c2 * 512:(c2 + 1) * 512], Bkr)
                    ki, Bki = ki_r.get()
                    ld(ki, Kspec[o, 1, fi, :, c2 * 512:(c2 + 1) * 512], Bki)
                    psR, BpR = PS()
                    for jc in range(16):
                        mm(psR[:, :], fre[:, jc, :], ztok[:, jc, :], jc == 0, jc == 15, [Bfre, Bztok], [BpR])
                    psI, BpI = PS()
                    for jc in range(16):
                        mm(psI[:, :], fim[:, jc, :], ztok[:, jc, :], jc == 0, jc == 15, [Bfim, Bztok], [BpI])
                    t1, Bt1 = t_r.get()
                    t2, Bt2 = t_r.get()
                    tt("dve", t1, psR[:, :], kr, ALU.mult, [BpR, Bkr], [Bt1])
                    tt("dve", t2, psI[:, :], ki, ALU.mult, [BpI, Bki], [Bt2])
                    tt("pool", Y[:, fi, :], t1, t2, ALU.subtract, [Bt1, Bt2], [BY])
                    t3, Bt3 = t_r.get()
                    t4_, Bt4 = t_r.get()
                    tt("dve", t3, psR[:, :], ki, ALU.mult, [BpR, Bki], [Bt3])
                    tt("dve", t4_, psI[:, :], kr, ALU.mult, [BpI, Bkr], [Bt4])
                    tt("pool", Y[:, 16 + fi, :], t3, t4_, ALU.add, [Bt3, Bt4], [BY])
                    if fi == 0:
                        tt("dve", Y[0:1, 16, :], psI[0:1, :], knyq[0:1, o, c2 * 512:(c2 + 1) * 512], ALU.mult, [BpI, Bknyq, BY], [BY])
                for t8 in range(8):
                    fm, Bfm = fmr.get()
                    ld(fm, Fms[t8], Bfm)
                    tsl = slice(t8 * 256, (t8 + 1) * 256)
                    for i in range(4):
                        ps, Bp = PS()
                        for rc in range(32):
                            mm(ps[:, 0:256], Y[:, rc, i * 128:(i + 1) * 128], fm[:, rc, :], rc == 0, rc == 31, [BY, Bfm], [Bp])
                        ev, Bev = e_r.get()
                        skc = COLV["skip"][0] + o * 8 + c2 * 4 + i
                        stt("dve", ev, zT[i][0][:, tsl], colv[:, skc:skc + 1], ps[:, 0:256], ALU.mult, ALU.add, [zT[i][1], Bcolv, Bp], [Bev])
                        if o == 0:
                            tt("pool", zT[i][0][:, tsl], ev, x1T[:, i, tsl], ALU.mult, [Bev, Bx1], [zT[i][1]])
                        else:
                            tt("pool", hh[:, i, tsl], ev, x2T[:, i, tsl], ALU.mult, [Bev, Bx2], [Bhh])
            for i in range(4):
                P.dma("pool", lambda e, i=i, c2=c2: e.dma_start(out=hhT_s[c2 * 4 + i], in_=hh[:, i, :]), Bhh, reads=[Bhh])
        P.barrier()
        aoff[0] = m0

    out_dmas = []

    def rms_rstd(xt_, Bx_, TT, sqring, rs, Brs):
        ps, Bp = PS()
        for kc in range(8):
            sq, Bsq = sqring.get()
            act(sq, xt_[:, kc, :], AF.Square, [Bx_], [Bsq])
            mm(ps[:, 0:TT], onesf, sq, kc == 0, kc == 7, [Bones, Bsq], [Bp])
        act(rs, ps[:, 0:TT], AF.Sqrt, [Bp, Beps], [Brs], bias=epsT[:, 0:1], scale=1.0 / D)
        P.op("dve", lambda e: e.reciprocal(out=rs, in_=rs), [Brs], [Brs])

    def stage45(b, j):
        m0 = aoff[0]
        TT = 512
        NT = L // TT
        wpm, Bwpm = tile(BF16, [128, 8, D])
        wph, Bwph = tile(BF16, [128, 8, D])
        wo_, Bwo = tile(BF16, [128, 8, D])
        ld(wpm, Wpm[:, :, :], Bwpm)
        ld(wph, Wph[:, :, :], Bwph)
        ld(wo_, Wout[:, :, :], Bwo)
        inr = {n_: (view(BF16, [128, 8, TT]), [Buf("in") for _ in range(8)]) for n_ in ("hm", "hh", "gm", "gh")}
        xr = ring(2, F32, [128, 8, TT])
        sqr = ring(2, F32, [128, TT])
        rsr = ring(2, F32, [128, TT])
        tmr = ring(4, F32, [128, TT])
        ypr = ring(1, BF16, [128, 8, TT])
        ur = ring(1, BF16, [128, NFC, TT])
        w13r = ring(4, BF16, [128, 8, 128])
        w2r = ring(2, BF16, [128, NFC, 128])
        outr = ring(2, F32, [128, TT])
        st_ = {}

        def loads(t_):
            tsl = slice(t_ * TT, (t_ + 1) * TT)
            tl = {}
            for n_, srcd, base in (("hm", hmT_s, 0), ("hh", hhT_s, 0), ("gm", fmT_s, 40), ("gh", fmT_s, 48)):
                tl[n_] = inr[n_]
                for kc in range(8):
                    P.dma("sp", lambda e, dst=tl[n_][0], srcd=srcd, base=base, kc=kc, tsl=tsl: e.dma_start(out=dst[:, kc, :], in_=srcd[base + kc, :, tsl]),
                          tl[n_][1][kc], writes=[tl[n_][1][kc]])
            st_.setdefault(t_, {})
            st_[t_]["tl"] = tl
            st_[t_]["tsl"] = tsl

        def loadx(t_):
            tsl = st_[t_]["tsl"] if t_ in st_ else slice(t_ * TT, (t_ + 1) * TT)
            xt, Bx = xr.get()
            ld(xt, xT[b][:, tsl].rearrange("(kc p) t -> p kc t", p=128), Bx)
            st_.setdefault(t_, dict(tsl=tsl))
            st_[t_]["xt"] = xt
            st_[t_]["Bx"] = Bx

        def phaseA(t_):
            d_ = st_[t_]
            tl, xt, Bx = d_["tl"], d_["xt"], d_["Bx"]
            yp, Byp = ypr.get()
            for cc in range(8):
                csl = slice(cc * 128, (cc + 1) * 128)
                ps, Bp = PS()
                for kc in range(8):
                    mm(ps[:, :], wpm[:, kc, csl], tl["hm"][0][:, kc, :], kc == 0, kc == 7, [Bwpm, tl["hm"][1][kc]], [Bp])
                ps2, Bp2 = PS()
                for kc in range(8):
                    mm(ps2[:, :], wph[:, kc, csl], tl["hh"][0][:, kc, :], kc == 0, kc == 7, [Bwph, tl["hh"][1][kc]], [Bp2])
                tm, Btm = tmr.get()
                tt("dve", tm, ps[:, :], tl["gm"][0][:, cc, :], ALU.mult, [Bp, tl["gm"][1][cc]], [Btm])
                tm2, Btm2 = tmr.get()
                tt("dve", tm2, ps2[:, :], tl["gh"][0][:, cc, :], ALU.mult, [Bp2, tl["gh"][1][cc]], [Btm2])
                tt("pool", yp[:, cc, :], tm, tm2, ALU.add, [Btm, Btm2], [Byp])
            if t_ + 1 < NT:
                loads(t_ + 1)
            for cc in range(8):
                csl = slice(cc * 128, (cc + 1) * 128)
                ps, Bp = PS()
                for kc in range(8):
                    mm(ps[:, :], wo_[:, kc, csl], yp[:, kc, :], kc == 0, kc == 7, [Bwo, Byp], [Bp])
                stt("dve", xt[:, cc, :], ps[:, :], modT[:, 16 + cc, j:j + 1], xt[:, cc, :], ALU.mult, ALU.add, [Bp, Bmod, Bx], [Bx])
            rs, Brs = rsr.get()
            rms_rstd(xt, Bx, TT, sqr, rs, Brs)
            h2, Bh2 = yp, Byp
            for kc in range(8):
                tm, Btm = tmr.get()
                stt("dve", tm, xt[:, kc, :], s2T[:, kc, j:j + 1], rs, ALU.mult, ALU.mult, [Bx, Bs2, Brs], [Btm])
                act(h2[:, kc, :], tm, AF.Identity, [Btm, Bmod], [Bh2], bias=modT[:, 24 + kc, j:j + 1])
            d_["h2"] = (h2, Bh2)

        def phaseB(t_):
            d_ = st_[t_]
            h2, Bh2 = d_["h2"]
            if t_ + 1 < NT:
                loadx(t_ + 1)
            u, Bu = ur.get()
            for fc in range(NFC):
                wa, Bwa = w13r.get()
                ld(wa, W1[fc], Bwa)
                wb, Bwb = w13r.get()
                ld(wb, W3[fc], Bwb)
                psa, Bpa = PS()
                for kc in range(8):
                    mm(psa[:, :], wa[:, kc, :], h2[:, kc, :], kc == 0, kc == 7, [Bwa, Bh2], [Bpa])
                psb, Bpb = PS()
                for kc in range(8):
                    mm(psb[:, :], wb[:, kc, :], h2[:, kc, :], kc == 0, kc == 7, [Bwb, Bh2], [Bpb])
                tm, Btm = tmr.get()
                act(tm, psa[:, :], AF.Silu, [Bpa], [Btm])
                tt("dve", u[:, fc, :], psb[:, :], tm, ALU.mult, [Bpb, Btm], [Bu])
            d_["u"] = (u, Bu)

        def phaseC(t_):
            d_ = st_[t_]
            xt, Bx, tsl = d_["xt"], d_["Bx"], d_["tsl"]
            u, Bu = d_["u"]
            for cc in range(8):
                w2t_, Bw2t = w2r.get()
                ld(w2t_, W2[cc], Bw2t)
                ps, Bp = PS()
                for fc in range(NFC):
                    mm(ps[:, :], w2t_[:, fc, :], u[:, fc, :], fc == 0, fc == NFC - 1, [Bw2t, Bu], [Bp])
                stt("dve", xt[:, cc, :], ps[:, :], modT[:, 40 + cc, j:j + 1], xt[:, cc, :], ALU.mult, ALU.add, [Bp, Bmod, Bx], [Bx])
            rs, Brs = rsr.get()
            rms_rstd(xt, Bx, TT, sqr, rs, Brs)
            for cc in range(8):
                ot_, Bo_ = outr.get()
                stt("dve", ot_, xt[:, cc, :], cv("final_g", cc), rs, ALU.mult, ALU.mult, [Bx, Bcolv, Brs], [Bo_])
                out_dmas.append(stq(outT[b, cc * 128:(cc + 1) * 128, tsl], ot_, Bo_, [Bo_]))

        loads(0)
        loadx(0)
        phaseA(0)
        for t_ in range(NT):
            phaseB(t_)
            if t_ + 1 < NT:
                phaseA(t_ + 1)
            phaseC(t_)
        P.barrier()
        aoff[0] = m0

    flush_casts(10 ** 6)
    P.barrier(include_deferred=True)
    for b in range(NB):
        if stop_after >= 2:
            stage1(ctxT[b], CL, 4, True)
            stage1(xT[b], L, b, False)
        if stop_after >= 3:
            stage2()
        if stop_after >= 4:
            stage3()
        if stop_after >= 5:
            stage45(b, b)
    P.barrier()
    P.emit(final_wait_ops=out_dmas[-8:])
    st.close()
    return nc


def _cols(v):
    v = np.asarray(v, np.float32).reshape(-1, 128)
    return np.ascontiguousarray(v.T)


def host_constants():
    n = L
    N = NFFT
    s = np.arange(n, dtype=np.float64)
    f = np.arange(n, dtype=np.float64)
    ang = 2.0 * np.pi * np.outer(s, f) / N
    FT = np.empty((n, N), np.float64)
    FT[:, :n] = np.cos(ang)
    FT[:, n:] = -np.sin(ang)
    FT[:, n] = np.cos(np.pi * s)
    FTb = FT.astype(ml_dtypes.bfloat16)
    FTs = np.ascontiguousarray(FTb.reshape(16, 128, 32, 128).transpose(2, 1, 0, 3))
    Fm = FTb.T
    Fms = np.ascontiguousarray(Fm.reshape(32, 128, 8, 256).transpose(2, 1, 0, 3))
    t = np.linspace(0.0, 1.0, n, dtype=np.float32)[:, None]
    bands = np.linspace(1e-4, 15.0, 16, dtype=np.float32)
    angz = (np.float32(2.0 * math.pi / n)) * np.arange(n, dtype=np.float32)[:, None] * bands[None, :]
    z = np.concatenate([t, np.cos(angz), -np.sin(angz)], axis=-1).astype(np.float32)
    zT = np.ascontiguousarray(z.T)
    tvec = np.ascontiguousarray(t[:, 0].reshape(16, 128).T)
    return FTs, Fms, zT, tvec


_CONST_CACHE = {}


def make_in_maps(inputs, NB=NB_FULL, ncores=NCORES):
    if "c" not in _CONST_CACHE:
        _CONST_CACHE["c"] = host_constants()
    FTs, Fms, zT, tvec = _CONST_CACHE["c"]
    g = {k: np.asarray(v) for k, v in inputs.items()}
    b_in = g["b_in"][0]
    colv = np.zeros((128, NCOLV), np.float32)

    def put(name, arr):
        o, w = COLV[name]
        assert arr.shape == (128, w), (name, arr.shape)
        colv[:, o:o + w] = arr

    put("b_ada", _cols(g["b_ada"][0]))
    put("norm1_g", _cols(g["norm1_g"][0]))
    put("norm2_g", _cols(g["norm2_g"][0]))
    put("final_g", _cols(g["final_g"]))
    put("b_fm", np.concatenate([_cols(b_in[fm_src_col(cc):fm_src_col(cc) + 128]) for cc in range(56)], axis=1))
    convw = np.concatenate([g["kq_conv_w"][0][:, :1024], g["kq_conv_w"][0][:, 1024:], g["hy_conv_w"][0]], axis=1)
    convb = np.concatenate([g["kq_conv_b"][0][:1024], g["kq_conv_b"][0][1024:], g["hy_conv_b"][0]])
    cw = np.zeros((128, 40, 3), np.float32)
    for jt in range(3):
        cw[:, :, jt] = _cols(convw[jt])
    put("cw", cw.reshape(128, 120))
    put("cb", _cols(convb))
    put("skip", _cols(g["hy_skip"][0].reshape(-1)))
    rowv = np.zeros((NROWV,), np.float32)

    def putr(name, arr):
        o, w = ROWV[name]
        rowv[o:o + w] = arr

    putr("bv", b_in[V_OFF:V_OFF + 1024])
    putr("bo", b_in[O_OFF:O_OFF + 1024])
    putr("bg", b_in[G_OFF:G_OFF + 16])
    putr("mng", g["m_norm_g"][0])
    rowv = np.ascontiguousarray(np.broadcast_to(rowv[None, :], (128, NROWV)))
    hyv = np.stack([g["hy_b1"][0], g["hy_b2"][0], g["hy_b3"][0], g["hy_freq"][0]], axis=1).astype(np.float32)
    shared = {
        "w_ada": g["w_ada"][0], "w_in": g["w_in"][0], "w_pm": g["w_pm"][0], "w_ph": g["w_ph"][0],
        "w_out": g["w_out"][0], "ffn_w1": g["ffn_w1"][0], "ffn_w3": g["ffn_w3"][0], "ffn_w2": g["ffn_w2"][0],
        "hy_w1": g["hy_w1"][0], "hy_w2": g["hy_w2"][0], "hy_w3": g["hy_w3"][0], "hy_w4": g["hy_w4"][0],
        "hyv": hyv, "decay_bc": np.broadcast_to(g["hy_decay"][0].reshape(1, -1), (128, 4096)), "colv": colv, "rowv": rowv, "FTs": FTs, "Fms": Fms, "zT": zT, "tvec": tvec,
    }
    shared = {k: np.ascontiguousarray(v) for k, v in shared.items()}
    maps = []
    for c in range(ncores):
        b0 = c * NB
        m = dict(shared)
        m["xT"] = np.ascontiguousarray(g["x"][b0:b0 + NB].transpose(0, 2, 1))
        m["ctxT"] = np.ascontiguousarray(g["ctx"][b0:b0 + NB].transpose(0, 2, 1))
        cc_ = np.zeros((5, D), np.float32)
        cc_[:NB] = g["c"][b0:b0 + NB]
        cc_[4] = g["c_ctx"]
        m["cT"] = np.ascontiguousarray(cc_.reshape(5, 8, 128).transpose(2, 1, 0))
        maps.append(m)
    return maps


_NC_CACHE = {}


def kernel(**inputs):
    if "nc" not in _NC_CACHE:
        _NC_CACHE["nc"] = build_program(NB_FULL)
    nc = _NC_CACHE["nc"]
    maps = make_in_maps(inputs)
    res = run_bass_kernel_spmd(nc, maps, core_ids=list(range(NCORES)))
    outs = [np.asarray(r["outT"]) for r in res.results]
    full = np.concatenate(outs, axis=0).transpose(0, 2, 1)
    return np.ascontiguousarray(full.astype(np.float32))
```

```python
import math
from contextlib import ExitStack
import numpy as np
import ml_dtypes
import concourse.bass as bass
import concourse.mybir as mybir
from concourse.bass_utils import run_bass_kernel_spmd

F32 = mybir.dt.float32
BF16 = mybir.dt.bfloat16
AF = mybir.ActivationFunctionType
ALU = mybir.AluOpType
AX = mybir.AxisListType

D = 1024
L = 2048
CL = 256
NH = 4
DH = 256
DFF = 2816
NFC = DFF // 128
NIN = 9232
K_OFF, V_OFF, G_OFF = 0, 1024, 2048
Q_OFF = G_OFF + 16
O_OFF = Q_OFF + 1024
HY_OFF = O_OFF + 1024
MG_OFF = HY_OFF + 3072
EPS = 1e-6
NFFT = 4096
NCORES = 8
NB_FULL = 4


class SemSlot:
    __slots__ = ("sem", "count")

    def __init__(self):
        self.sem = None
        self.count = 0


class Buf:
    __slots__ = ("name", "last_w", "readers", "slot", "epoch", "last_dma")

    def __init__(self, name):
        self.name = name
        self.last_w = None
        self.readers = []
        self.slot = None
        self.epoch = -1
        self.last_dma = None


class Op:
    __slots__ = ("eng", "fn", "deps", "is_dma", "owner", "sig", "idx", "group")

    def __init__(self, eng, fn, is_dma=False, owner=None):
        self.eng = eng
        self.fn = fn
        self.deps = set()
        self.is_dma = is_dma
        self.owner = owner
        self.sig = None
        self.idx = None
        self.group = None


class Prog:
    ENGS = ("pe", "act", "dve", "pool", "sp")

    def __init__(self, nc):
        self.nc = nc
        self.ops = []
        self.last_on = {e: None for e in self.ENGS}
        self.live_dma = {}
        self.deferred_dma = {}
        self.bar = None
        self.bar_seen = set()
        self.epoch = 0
        self.free_slots = {e: [] for e in self.ENGS}
        self.used_slots = []
        self.all_slots = []

    def buf(self, name="b"):
        return Buf(name)

    def _add(self, op, reads, writes):
        op.idx = len(self.ops)
        self.ops.append(op)
        for b in reads:
            if b.last_w is not None:
                op.deps.add(b.last_w)
        for b in writes:
            if b.last_w is not None:
                op.deps.add(b.last_w)
            op.deps.update(b.readers)
        for b in reads:
            b.readers.append(op.idx)
        for b in writes:
            b.last_w = op.idx
            b.readers = []
        op.deps.discard(op.idx)
        if self.bar is not None and op.eng not in self.bar_seen:
            op.deps.add(self.bar)
            self.bar_seen.add(op.eng)
        if not op.is_dma:
            self.last_on[op.eng] = op.idx
        return op

    def op(self, eng, fn, reads=(), writes=()):
        return self._add(Op(eng, fn), reads, writes)

    def mm(self, fn, reads=(), writes=()):
        return self._add(Op("pe", fn), reads, writes)

    def dma(self, eng, fn, owner, reads=(), writes=(), deferred=False, group=None):
        o = Op(eng, fn, is_dma=True, owner=owner)
        o.group = group
        self._add(o, reads, writes)
        if owner.slot is None:
            owner.slot = {}
        if deferred:
            if eng not in owner.slot:
                sl = SemSlot()
                self.all_slots.append(sl)
                owner.slot[eng] = (sl, -1)
        elif eng not in owner.slot or owner.slot[eng][1] != self.epoch:
            if self.free_slots[eng]:
                sl = self.free_slots[eng].pop()
            else:
                sl = SemSlot()
                self.all_slots.append(sl)
            self.used_slots.append((eng, sl))
            owner.slot[eng] = (sl, self.epoch)
        if owner.epoch != self.epoch and not deferred:
            owner.epoch = self.epoch
            owner.last_dma = None
        slot = owner.slot[eng][0]
        if owner.last_dma is not None:
            prev = self.ops[owner.last_dma]
            if group is not None and prev.group == group:
                o.deps.discard(prev.idx)
                o.deps.update(prev.deps)
            else:
                o.deps.add(owner.last_dma)
        owner.last_dma = o.idx
        slot.count += 1
        o.sig = (slot, 16 * slot.count)
        if deferred:
            self.deferred_dma[(id(owner), eng)] = o.idx
        else:
            self.live_dma[(id(owner), eng)] = o.idx
        return o

    def barrier(self, include_deferred=False):
        if include_deferred:
            self.live_dma.update(self.deferred_dma)
            self.deferred_dma = {}
        o = Op("sp", lambda e: e.nop())
        o.idx = len(self.ops)
        self.ops.append(o)
        for e in self.ENGS:
            if self.last_on[e] is not None:
                o.deps.add(self.last_on[e])
        o.deps.update(self.live_dma.values())
        if self.bar is not None:
            o.deps.add(self.bar)
        self.live_dma = {}
        for e_, sl_ in self.used_slots:
            self.free_slots[e_].append(sl_)
        self.used_slots = []
        self.epoch += 1
        self.bar = o.idx
        self.bar_seen = {"sp"}
        self.last_on["sp"] = o.idx
        return o

    def emit(self, final_wait_ops=()):
        nc = self.nc
        ops = self.ops
        needed = set()
        for o in ops:
            needed.update(o.deps)
        for o in final_wait_ops:
            needed.add(o.idx)
        eng_cnt = {e: 0 for e in self.ENGS}
        dma_bufs = self.all_slots
        for o in ops:
            if o.is_dma:
                pass
            elif o.idx in needed:
                eng_cnt[o.eng] += 1
                o.sig = (o.eng, eng_cnt[o.eng])
        self.n_sems = len(dma_bufs) + len(self.ENGS)
        with ExitStack() as st:
            esem = {e: st.enter_context(nc.semaphore("s_" + e)) for e in self.ENGS}
            for i, b in enumerate(dma_bufs):
                b.sem = st.enter_context(nc.semaphore("d%d" % i))
            block = st.enter_context(nc.Block())

            def sem_of(key):
                return esem[key] if isinstance(key, str) else key.sem

            per_eng = {e: [o for o in ops if o.eng == e] for e in self.ENGS}
            final = list(final_wait_ops)

            def run_engine(ename, eng):
                waited = {}
                for o in per_eng[ename]:
                    need = {}
                    for d in o.deps:
                        do = ops[d]
                        if do.sig is None:
                            continue
                        if (not do.is_dma) and do.eng == ename and ename in ("pe", "sp"):
                            continue
                        key, val = do.sig
                        kid = key if isinstance(key, str) else id(key)
                        if need.get(kid, (None, 0))[1] < val:
                            need[kid] = (key, val)
                    for kid, (key, val) in need.items():
                        if waited.get(kid, 0) >= val:
                            continue
                        eng.wait_ge(sem_of(key), val)
                        waited[kid] = val
                    ins = o.fn(eng)
                    if o.sig is not None:
                        key, val = o.sig
                        ins.then_inc(sem_of(key), 16 if o.is_dma else 1)
                if ename == "sp":
                    for o in final:
                        key, val = o.sig
                        eng.wait_ge(sem_of(key), val)

            block.sync(lambda e: run_engine("sp", e))
            block.tensor(lambda e: run_engine("pe", e))
            block.scalar(lambda e: run_engine("act", e))
            block.vector(lambda e: run_engine("dve", e))
            block.gpsimd(lambda e: run_engine("pool", e))


class Ring:
    def __init__(self, views):
        self.items = [(v, Buf("r")) for v in views]
        self.i = 0

    def get(self):
        it = self.items[self.i % len(self.items)]
        self.i += 1
        return it


COLV = {}
_o = 0
for _n, _w in (("b_ada", 48), ("norm1_g", 8), ("norm2_g", 8), ("final_g", 8), ("b_fm", 56),
               ("cw", 120), ("cb", 40), ("skip", 16)):
    COLV[_n] = (_o, _w)
    _o += _w
NCOLV = _o
ROWV = {}
_o = 0
for _n, _w in (("bv", 1024), ("bo", 1024), ("bg", 16), ("mng", 1024)):
    ROWV[_n] = (_o, _w)
    _o += _w
NROWV = _o


def fm_src_col(cc):
    if cc < 8:
        return K_OFF + cc * 128
    if cc < 16:
        return Q_OFF + (cc - 8) * 128
    if cc < 40:
        return HY_OFF + (cc - 16) * 128
    return MG_OFF + (cc - 40) * 128


def build_program(NB=NB_FULL, debug=(), stop_after=99):
    nc = bass.Bass("TRN2", target_bir_lowering=False)
    P = Prog(nc)

    def din(name, shape, dt=F32):
        return nc.dram_tensor(name, list(shape), dt, kind="ExternalInput").ap()

    def dscr(name, shape, dt):
        kind = "ExternalOutput" if name in debug else "Internal"
        return nc.dram_tensor(name, list(shape), dt, kind=kind).ap()

    xT = din("xT", [NB, D, L])
    ctxT = din("ctxT", [NB, D, CL])
    cT = din("cT", [128, 8, 5])
    w_ada = din("w_ada", [D, 6 * D])
    w_in = din("w_in", [D, NIN])
    w_pm = din("w_pm", [D, D])
    w_ph = din("w_ph", [D, D])
    w_out = din("w_out", [D, D])
    ffn_w1 = din("ffn_w1", [D, DFF])
    ffn_w3 = din("ffn_w3", [D, DFF])
    ffn_w2 = din("ffn_w2", [DFF, D])
    hy_w1 = din("hy_w1", [33, 64])
    hy_w2 = din("hy_w2", [64, 64])
    hy_w3 = din("hy_w3", [64, 64])
    hy_w4 = din("hy_w4", [64, 4096])
    hyv_d = din("hyv", [64, 4])
    colv_d = din("colv", [128, NCOLV])
    rowv_d = din("rowv", [128, NROWV])
    FTs = din("FTs", [32, 128, 16, 128], BF16)
    Fms = din("Fms", [8, 128, 32, 256], BF16)
    zT_d = din("zT", [33, L])
    tvec_d = din("tvec", [128, 16])
    decay_d = din("decay_bc", [128, 4096])
    outT = nc.dram_tensor("outT", [NB, D, L], F32, kind="ExternalOutput").ap()

    Wfm = dscr("Wfm", [56, 128, 8, 128], BF16)
    Wtm = dscr("Wtm", [8, 128, 8, 256], BF16)
    Wg = dscr("Wg", [128, 8, 16], BF16)
    Wpm = dscr("Wpm", [128, 8, D], BF16)
    Wph = dscr("Wph", [128, 8, D], BF16)
    Wout = dscr("Wout", [128, 8, D], BF16)
    W1 = dscr("W1", [NFC, 128, 8, 128], BF16)
    W3 = dscr("W3", [NFC, 128, 8, 128], BF16)
    W2 = dscr("W2", [8, 128, NFC, 128], BF16)
    Kspec = dscr("Kspec", [2, 2, 16, 128, D], F32)
    fmT_s = dscr("fmT_s", [56, 128, L], BF16)
    kTc_s = dscr("kTc_s", [8, 128, CL], BF16)
    v_s = dscr("v_s", [4, 128, 16, 256], BF16)
    o_s = dscr("o_s", [4, 128, 16, 256], BF16)
    vc_s = dscr("vc_s", [4, 128, 2, 256], BF16)
    g_s = dscr("g_s", [128, 18, 16], F32)
    hmT_s = dscr("hmT_s", [8, 128, L], BF16)
    hhT_s = dscr("hhT_s", [8, 128, L], BF16)

    st = ExitStack()
    ARENA_BF = 106000
    arena = st.enter_context(nc.sbuf_tensor("arena", [128, ARENA_BF], BF16))
    psum = [(st.enter_context(nc.psum_tensor("ps%d" % i, [128, 512], F32)), Buf("ps")) for i in range(8)]
    psi = [0]

    def PS():
        it = psum[psi[0] % 8]
        psi[0] += 1
        return it

    aoff = [0]

    def view(dt, shape, np_=128):
        n = int(np.prod(shape[1:]))
        nb = n * 2 if dt == F32 else n
        nb = (nb + 15) // 16 * 16
        assert aoff[0] + nb <= ARENA_BF, ("SBUF overflow", aoff[0], nb)
        a = arena[0:shape[0], aoff[0]:aoff[0] + nb]
        aoff[0] += nb
        if dt == F32:
            a = a.bitcast(F32)
        a = a[:, 0:n]
        if len(shape) == 3:
            a = a.rearrange("p (a b) -> p a b", a=shape[1])
        elif len(shape) == 4:
            a = a.rearrange("p (a b c) -> p a b c", a=shape[1], b=shape[2])
        return a

    def tile(dt, shape):
        return view(dt, shape), Buf("t")

    def ring(n, dt, shape):
        return Ring([view(dt, shape) for _ in range(n)])

    def act(out, in_, func, reads, writes, bias=None, scale=None):
        kw = {}
        if bias is not None:
            kw["bias"] = bias
        if scale is not None:
            kw["scale"] = scale
        return P.op("act", lambda e: e.activation(out=out, in_=in_, func=func, **kw), reads, writes)

    def tt(eng, out, in0, in1, op, reads, writes):
        return P.op(eng, lambda e: e.tensor_tensor(out=out, in0=in0, in1=in1, op=op), reads, writes)

    def ts(eng, out, in0, s1, op0, reads, writes, s2=None, op1=None):
        if op1 is None:
            return P.op(eng, lambda e: e.tensor_scalar(out=out, in0=in0, scalar1=s1, scalar2=None, op0=op0), reads, writes)
        return P.op(eng, lambda e: e.tensor_scalar(out=out, in0=in0, scalar1=s1, scalar2=s2, op0=op0, op1=op1), reads, writes)

    def stt(eng, out, in0, scalar, in1, op0, op1, reads, writes):
        return P.op(eng, lambda e: e.scalar_tensor_tensor(out=out, in0=in0, scalar=scalar, in1=in1, op0=op0, op1=op1), reads, writes)

    def cp(eng, out, in_, reads, writes):
        if eng == "act":
            return act(out, in_, AF.Copy, reads, writes)
        return P.op(eng, lambda e: e.tensor_copy(out=out, in_=in_), reads, writes)

    def mm(out, lhsT, rhs, start, stop, reads, writes):
        return P.mm(lambda e: e.matmul(out, lhsT=lhsT, rhs=rhs, start=start, stop=stop), reads, writes)

    def tr(out, in_, reads, writes):
        return P.mm(lambda e: e.transpose(out=out, in_=in_, identity=ident), reads, writes)

    def ld(out, in_, owner, reads=(), q="sp"):
        return P.dma(q, lambda e: e.dma_start(out=out, in_=in_), owner, reads=reads, writes=[owner])

    def stq(out, in_, owner, reads, q="pool"):
        return P.dma(q, lambda e: e.dma_start(out=out, in_=in_), owner, reads=reads)

    colv, Bcolv = tile(F32, [128, NCOLV])
    rowv, Browv = tile(F32, [128, NROWV])
    ident, Bident = tile(BF16, [128, 128])
    maskF, BmaskF = tile(F32, [128, 128])
    maskB, BmaskB = tile(F32, [128, 128])
    onesf, Bones = tile(F32, [128, 128])
    epsT, Beps = tile(F32, [128, 1])
    modT, Bmod = tile(F32, [128, 48, 5])
    s1T, Bs1 = tile(F32, [128, 8, 5])
    s2T, Bs2 = tile(F32, [128, 8, 5])
    knyq, Bknyq = tile(F32, [1, 2, D])
    tvec, Btvec = tile(F32, [128, 16])
    w1b, Bw1b = tile(F32, [128, 40])
    CONST = [Bcolv, Browv, Bident, BmaskF, BmaskB, Bones, Beps, Bmod, Bs1, Bs2, Bknyq, Btvec]
    persist_mark = aoff[0]

    def cv(name, i=0, n=1):
        o, w = COLV[name]
        return colv[:, o + i:o + i + n]

    def rv(name, i=0, n=None):
        o, w = ROWV[name]
        n = w if n is None else n
        return rowv[:, o + i:o + i + n]

    ld(colv, colv_d[:, :], Bcolv)
    ld(rowv, rowv_d[:, :], Browv)
    ld(tvec, tvec_d[:, :], Btvec)
    P.op("pool", lambda e: e.memset(onesf, 1.0), (), [Bones])
    P.op("pool", lambda e: e.memset(epsT, EPS), (), [Beps])
    P.op("pool", lambda e: e.memset(maskF, 1.0), (), [BmaskF])
    P.op("pool", lambda e: e.memset(maskB, 1.0), (), [BmaskB])
    P.op("pool", lambda e: e.affine_select(out=maskF, in_=maskF, pattern=[[1, 128]], compare_op=ALU.is_ge, fill=0.0, base=0, channel_multiplier=-1), [BmaskF], [BmaskF])
    P.op("pool", lambda e: e.affine_select(out=maskB, in_=maskB, pattern=[[-1, 128]], compare_op=ALU.is_ge, fill=0.0, base=0, channel_multiplier=1), [BmaskB], [BmaskB])
    idf, Bidf = tile(F32, [128, 128])
    P.op("pool", lambda e: e.memset(idf, 1.0), (), [Bidf])
    P.op("pool", lambda e: e.affine_select(out=idf, in_=idf, pattern=[[-1, 128]], compare_op=ALU.is_equal, fill=0.0, base=0, channel_multiplier=1), [Bidf], [Bidf])
    cp("dve", ident, idf, [Bidf], [Bident])
    _cwv = colv[:, COLV["cw"][0]:COLV["cw"][0] + 120].rearrange("p (c j) -> p c j", j=3)
    tt("dve", w1b, _cwv[:, :, 1], cv("b_fm", 0, 40), ALU.mult, [Bcolv], [Bw1b])

    cast_owner = [Buf("cast%d" % i) for i in range(8)]
    cidx = [0]
    pending_casts = []

    def cast(out, in_):
        pending_casts.append((out, in_))

    def flush_casts(n):
        for _ in range(min(n, len(pending_casts))):
            out, in_ = pending_casts.pop(0)
            b = cast_owner[cidx[0] % len(cast_owner)]
            cidx[0] += 1
            P.dma("pool", lambda e, out=out, in_=in_: e.dma_start(out=out, in_=in_), b, deferred=True)

    def kview(w2d):
        return w2d.rearrange("(kc p) c -> p kc c", p=128)

    for cc in range(56):
        c0 = fm_src_col(cc)
        cast(Wfm[cc], kview(w_in[:, c0:c0 + 128]))
    for h in range(4):
        cast(Wtm[h], kview(w_in[:, V_OFF + h * 256:V_OFF + (h + 1) * 256]))
        cast(Wtm[4 + h], kview(w_in[:, O_OFF + h * 256:O_OFF + (h + 1) * 256]))
    cast(Wg[:, :, :], kview(w_in[:, G_OFF:G_OFF + 16]))
    for kc in range(8):
        cast(Wpm[:, kc, :], w_pm[kc * 128:(kc + 1) * 128, :])
        cast(Wph[:, kc, :], w_ph[kc * 128:(kc + 1) * 128, :])
        cast(Wout[:, kc, :], w_out[kc * 128:(kc + 1) * 128, :])
    for fc in range(NFC):
        cast(W1[fc], kview(ffn_w1[:, fc * 128:(fc + 1) * 128]))
        cast(W3[fc], kview(ffn_w3[:, fc * 128:(fc + 1) * 128]))
    for cc in range(8):
        cast(W2[cc], ffn_w2[:, cc * 128:(cc + 1) * 128].rearrange("(fc p) c -> p fc c", p=128))

    flush_casts(16)

    m0 = aoff[0]
    sc, Bsc = tile(F32, [128, 8, 5])
    ld(sc, cT[:, :, :], Bsc)
    act(sc, sc, AF.Silu, [Bsc], [Bsc])
    wr = ring(3, F32, [128, 8, 128])
    for oc in range(48):
        wt, Bw = wr.get()
        ld(wt, kview(w_ada[:, oc * 128:(oc + 1) * 128]), Bw)
        ps, Bp = PS()
        for kc in range(8):
            mm(ps[:, 0:5], wt[:, kc, :], sc[:, kc, :], kc == 0, kc == 7, [Bw, Bsc], [Bp])
        act(modT[:, oc, :], ps[:, 0:5], AF.Identity, [Bp, Bcolv], [Bmod], bias=cv("b_ada", oc))
    ts("dve", s1T, modT[:, 8:16, :], 1.0, ALU.add, [Bmod], [Bs1])
    tt("dve", s1T, s1T, cv("norm1_g", 0, 8).unsqueeze(2).to_broadcast([128, 8, 5]), ALU.mult, [Bs1, Bcolv], [Bs1])
    ts("dve", s2T, modT[:, 32:40, :], 1.0, ALU.add, [Bmod], [Bs2])
    tt("dve", s2T, s2T, cv("norm2_g", 0, 8).unsqueeze(2).to_broadcast([128, 8, 5]), ALU.mult, [Bs2, Bcolv], [Bs2])
    P.barrier()
    aoff[0] = m0

    if stop_after >= 1:
        m0 = aoff[0]
        aA, BaA = tile(F32, [64, L])
        w4t, Bw4 = tile(F32, [64, 4096])
        ld(w4t, hy_w4[:, :], Bw4)
        m1 = aoff[0]
        zt, Bzt = tile(F32, [33, L])
        w1t, Bw1 = tile(F32, [33, 64])
        w2t, Bw2 = tile(F32, [64, 64])
        w3t, Bw3 = tile(F32, [64, 64])
        hyv, Bhyv = tile(F32, [64, 4])
        fb, Bfb = tile(F32, [64, 3])
        ld(zt, zT_d[:, :], Bzt)
        ld(w1t, hy_w1[:, :], Bw1)
        ld(w2t, hy_w2[:, :], Bw2)
        ld(w3t, hy_w3[:, :], Bw3)
        ld(hyv, hyv_d[:, :], Bhyv)
        tt("dve", fb, hyv[:, 0:3], hyv[:, 3:4].to_broadcast([64, 3]), ALU.mult, [Bhyv], [Bfb])
        aB, BaB = tile(F32, [64, L])
        w1s, Bw1s = tile(F32, [64, L])
        w2s, Bw2s = tile(F32, [64, L])
        prev, Bprev = zt, Bzt
        for li, (wt_, Bwt_) in enumerate(((w1t, Bw1), (w2t, Bw2), (w3t, Bw3))):
            cur, Bcur = (aA, BaA) if li % 2 == 0 else (aB, BaB)
            for t4 in range(4):
                ps, Bp = PS()
                mm(ps[0:64, :], wt_, prev[:, t4 * 512:(t4 + 1) * 512], True, True, [Bwt_, Bprev], [Bp])
                act(cur[:, t4 * 512:(t4 + 1) * 512], ps[0:64, :], AF.Identity, [Bp, Bhyv, Bfb], [Bcur],
                    bias=fb[:, li:li + 1], scale=hyv[:, 3:4])
            ts("dve", w1s, cur, -math.pi, ALU.is_lt, [Bcur], [Bw1s], s2=2 * math.pi, op1=ALU.mult)
            ts("dve", w2s, cur, math.pi, ALU.is_gt, [Bcur], [Bw2s], s2=-2 * math.pi, op1=ALU.mult)
            tt("dve", cur, cur, w1s, ALU.add, [Bcur, Bw1s], [Bcur])
            tt("dve", cur, cur, w2s, ALU.add, [Bcur, Bw2s], [Bcur])
            act(cur, cur, AF.Sin, [Bcur], [Bcur])
            prev, Bprev = cur, Bcur
        a3, Ba3 = prev, Bprev
        assert a3 is aA
        P.barrier()
        aoff[0] = m1
        absd, Babsd = tile(F32, [128, 4096])
        ld(absd, decay_d[:, :], Babsd)
        act(absd, absd, AF.Abs, [Babsd], [Babsd])
        ntv, Bntv = tile(F32, [128, 16])
        ts("dve", ntv, tvec, -1.0, ALU.mult, [Btvec], [Bntv])
        Eall, BE = tile(BF16, [128, 16, 1024])
        Oall, BO = tile(BF16, [128, 16, 1024])
        winr = ring(2, F32, [128, 512])
        tapr = ring(2, F32, [128, 2048])
        ftr = ring(4, BF16, [128, 16, 128])
        kor = ring(4, F32, [128, 512])
        for o in range(2):
            for tc in range(16):
                flush_casts(2)
                taps, Btap = tapr.get()
                for cb in range(4):
                    gsl = slice(o * 2048 + cb * 512, o * 2048 + (cb + 1) * 512)
                    ps, Bp = PS()
                    mm(ps[:, :], a3[:, tc * 128:(tc + 1) * 128], w4t[:, gsl], True, True, [Ba3, Bw4], [Bp])
                    win, Bwin = winr.get()
                    act(win, absd[:, gsl], AF.Exp, [Babsd, Bntv], [Bwin], scale=ntv[:, tc:tc + 1])
                    tt("dve", taps[:, cb * 512:(cb + 1) * 512], ps[:, :], win, ALU.mult, [Bp, Bwin], [Btap])
                tt("pool", Eall[:, tc, :], taps[:, 0:1024], taps[:, 1024:2048], ALU.add, [Btap], [BE])
                tt("pool", Oall[:, tc, :], taps[:, 0:1024], taps[:, 1024:2048], ALU.subtract, [Btap], [BO])
            for fi in range(16):
                flush_casts(2)
                fre, Bfre = ftr.get()
                ld(fre, FTs[fi], Bfre)
                fim, Bfim = ftr.get()
                ld(fim, FTs[16 + fi], Bfim)
                for ch in range(2):
                    csl = slice(ch * 512, ch * 512 + 512)
                    psR, BpR = PS()
                    for jc in range(16):
                        mm(psR[:, :], fre[:, jc, :], Eall[:, jc, csl], jc == 0, jc == 15, [Bfre, BE], [BpR])
                    psI, BpI = PS()
                    for jc in range(16):
                        mm(psI[:, :], fim[:, jc, :], Oall[:, jc, csl], jc == 0, jc == 15, [Bfim, BO], [BpI])
                    kr, Bkr = kor.get()
                    act(kr, psR[:, :], AF.Copy, [BpR], [Bkr], scale=2.0 / NFFT)
                    ki, Bki = kor.get()
                    act(ki, psI[:, :], AF.Copy, [BpI], [Bki], scale=2.0 / NFFT)
                    if fi == 0:
                        ts("dve", kr[0:1, :], kr[0:1, :], 0.5, ALU.mult, [Bkr], [Bkr])
                        P.op("dve", lambda e, ki=ki: e.memset(ki[0:1, :], 0.0), [Bki], [Bki])
                        psN, BpN = PS()
                        for jc in range(16):
                            mm(psN[:, :], fim[:, jc, :], Eall[:, jc, csl], jc == 0, jc == 15, [Bfim, BE], [BpN])
                        act(knyq[0:1, o, ch * 512:(ch + 1) * 512], psN[0:1, :], AF.Copy, [BpN], [Bknyq], scale=1.0 / NFFT)
                    stq(Kspec[o, 0, fi, :, ch * 512:(ch + 1) * 512], kr, Bkr, [Bkr])
                    stq(Kspec[o, 1, fi, :, ch * 512:(ch + 1) * 512], ki, Bki, [Bki])
        P.barrier()
        aoff[0] = m0

    def stage1(src, T, j, is_ctx):
        m0 = aoff[0]
        TT = min(512, T)
        ntt = T // TT
        hT = view(BF16, [128, 8, T])
        BhTs = [Buf("hT") for _ in range(ntt)]
        xr = ring(2, F32, [128, 8, TT])
        sqr = ring(1, F32, [128, 8, TT])
        rsr = ring(2, F32, [128, TT])
        tmr = ring(2, F32, [128, TT])
        def norm(t_):
            tsl = slice(t_ * TT, (t_ + 1) * TT)
            xt, Bx = xr.get()
            ld(xt, src[:, tsl].rearrange("(kc p) t -> p kc t", p=128), Bx)
            sq, Bsq = sqr.get()
            act(sq, xt, AF.Square, [Bx], [Bsq])
            ps, Bp = PS()
            for kc in range(8):
                mm(ps[:, 0:TT], onesf, sq[:, kc, :], kc == 0, kc == 7, [Bones, Bsq], [Bp])
            rs, Brs = rsr.get()
            act(rs, ps[:, 0:TT], AF.Sqrt, [Bp, Beps], [Brs], bias=epsT[:, 0:1], scale=1.0 / D)
            P.op("dve", lambda e, rs=rs: e.reciprocal(out=rs, in_=rs), [Brs], [Brs])
            for kc in range(8):
                tm, Btm = tmr.get()
                stt("dve", tm, xt[:, kc, :], s1T[:, kc, j:j + 1], rs, ALU.mult, ALU.mult, [Bx, Bs1, Brs], [Btm])
                act(hT[:, kc, tsl], tm, AF.Identity, [Btm, Bmod], [BhTs[t_]], bias=modT[:, kc, j:j + 1])
        wr_ = ring(3, BF16, [128, 8, 128])
        pbr = ring(3, F32, [128, TT])
        yr = ring(3, F32, [128, TT])
        osr = ring(3, BF16, [128, TT])
        RW = 64 if not is_ctx else T
        NR = TT // RW
        cclist = range(8) if is_ctx else range(56)
        pending = []

        def finish():
            while pending:
                pending.pop(0)()

        def proj(cc, tts):
            wt, Bw = wr_.get()
            ld(wt, Wfm[cc], Bw)
            for t_ in tts:
                tsl = slice(t_ * TT, (t_ + 1) * TT)
                ps, Bp = PS()
                for kc in range(8):
                    mm(ps[:, 0:TT], wt[:, kc, :], hT[:, kc, tsl], kc == 0, kc == 7, [Bw, BhTs[t_]], [Bp])
                ost, Bos = osr.get()
                dst = kTc_s[cc, :, tsl] if is_ctx else fmT_s[cc, :, tsl]
                if cc < 40:
                    pb, Bpb = pbr.get()
                    act(pb, ps[:, 0:TT], AF.Identity, [Bp, Bcolv], [Bpb], bias=cv("b_fm", cc))
                    y, By = yr.get()
                    cwo = COLV["cw"][0] + cc * 3
                    act(y, ps[:, 0:TT], AF.Identity, [Bp, Bcolv, Bw1b], [By], bias=w1b[:, cc:cc + 1], scale=colv[:, cwo + 1:cwo + 2])
                    finish()
                    y3 = y.rearrange("p (r w) -> p r w", w=RW)
                    p3 = pb.rearrange("p (r w) -> p r w", w=RW)
                    stt("dve", y3[:, :, 1:RW], p3[:, :, 0:RW - 1], colv[:, cwo:cwo + 1], y3[:, :, 1:RW], ALU.mult, ALU.add, [Bpb, By, Bcolv], [By])
                    stt("dve", y3[:, :, 0:RW - 1], p3[:, :, 1:RW], colv[:, cwo + 2:cwo + 3], y3[:, :, 0:RW - 1], ALU.mult, ALU.add, [Bpb, By, Bcolv], [By])

                    def fin(ost=ost, Bos=Bos, y=y, By=By, cc=cc, dst=dst):
                        act(ost, y, AF.Silu if cc < 16 else AF.Identity, [By, Bcolv], [Bos], bias=cv("cb", cc))
                        stq(dst, ost, Bos, [Bos])
                    pending.append(fin)
                else:
                    finish()
                    act(ost, ps[:, 0:TT], AF.Sigmoid, [Bp, Bcolv], [Bos], bias=cv("b_fm", cc))
                    stq(dst, ost, Bos, [Bos])
        if is_ctx or ntt < 4:
            for t_ in range(ntt):
                norm(t_)
            for cc in cclist:
                proj(cc, list(range(ntt)))
        else:
            norm(0)
            norm(1)
            for cc in range(4):
                proj(cc, [0, 1])
            norm(2)
            norm(3)
            for cc in range(4):
                proj(cc, [2, 3])
            for cc in range(4, 56):
                proj(cc, [0, 1, 2, 3])
        finish()
        wtm, Bwtm = tile(BF16, [128, 8, 8, 256])
        for i in range(8):
            if is_ctx and i >= 4:
                continue
            P.dma("sp", lambda e, i=i: e.dma_start(out=wtm[:, i, :, :], in_=Wtm[i]), Bwtm, writes=[Bwtm])
        wg, Bwg = tile(BF16, [128, 8, 16])
        ld(wg, Wg[:, :, :], Bwg)
        ntc = T // 128
        gt, Bgt = tile(F32, [128, ntc, 16])
        vsr = ring(3, BF16, [128, 256])
        tmo = ring(2, F32, [128, 256])
        for tc in range(ntc):
            csl = slice(tc * 128, (tc + 1) * 128)
            for i in range(8):
                if is_ctx and i >= 4:
                    continue
                ps, Bp = PS()
                for kc in range(8):
                    mm(ps[:, 0:256], hT[:, kc, csl], wtm[:, i, kc, :], kc == 0, kc == 7, [BhTs[(tc * 128) // TT], Bwtm], [Bp])
                vs, Bvs = vsr.get()
                if i < 4:
                    tt("dve", vs, ps[:, 0:256], rv("bv", i * 256, 256), ALU.add, [Bp, Browv], [Bvs])
                    dst = vc_s[i, :, tc, :] if is_ctx else v_s[i, :, tc, :]
                else:
                    tm, Btm = tmo.get()
                    tt("dve", tm, ps[:, 0:256], rv("bo", (i - 4) * 256, 256), ALU.add, [Bp, Browv], [Btm])
                    act(vs, tm, AF.Sigmoid, [Btm], [Bvs])
                    dst = o_s[i - 4, :, tc, :]
                stq(dst, vs, Bvs, [Bvs])
            ps, Bp = PS()
            for kc in range(8):
                mm(ps[:, 0:16], hT[:, kc, csl], wg[:, kc, :], kc == 0, kc == 7, [BhTs[(tc * 128) // TT], Bwg], [Bp])
            tt("dve", gt[:, tc, :], ps[:, 0:16], rv("bg"), ALU.add, [Bp, Browv], [Bgt])
        if is_ctx:
            stq(g_s[:, 16:18, :], gt, Bgt, [Bgt])
        else:
            stq(g_s[:, 0:16, :], gt, Bgt, [Bgt])
        P.barrier()
        aoff[0] = m0

    def stage2():
        m0 = aoff[0]
        G, BG = tile(F32, [128, 18, 16])
        ld(G, g_s[:, :, :], BG)
        ax, Bax = tile(F32, [128, 18, 16])
        LS, BLS = tile(F32, [128, 18, 16])
        act(ax, G, AF.Abs, [BG], [Bax])
        act(ax, ax, AF.Exp, [Bax], [Bax], scale=-1.0)
        act(ax, ax, AF.Ln, [Bax], [Bax], bias=1.0)
        ts("dve", LS, G, 0.0, ALU.min, [BG], [BLS])
        tt("dve", LS, LS, ax, ALU.subtract, [BLS, Bax], [BLS])
        LS2 = LS.rearrange("p a b -> p (a b)")
        cumF, BcF = tile(F32, [128, 18, 16])
        cumB, BcB = tile(F32, [128, 18, 16])
        tot, Btot = tile(F32, [128, 18, 16])
        for (lh, Blh, dstt, Bd) in ((maskF, BmaskF, cumF, BcF), (maskB, BmaskB, cumB, BcB), (onesf, Bones, tot, Btot)):
            ps, Bp = PS()
            mm(ps[:, 0:288], lh, LS2, True, True, [Blh, BLS], [Bp])
            cp("act", dstt.rearrange("p a b -> p (a b)"), ps[:, 0:288], [Bp], [Bd])
        gate = {}
        for dname, cum, Bc, lcol, fcol in (("f", cumF, BcF, 0, 4), ("b", cumB, BcB, 8, 12)):
            A_, BA_ = tile(F32, [128, 18, 4])
            BK_, BBK_ = tile(F32, [128, 18, 4])
            DEC_, BDEC_ = tile(F32, [128, 18, 4])
            act(A_, cum[:, :, fcol:fcol + 4], AF.Exp, [Bc], [BA_])
            tt("dve", BK_, G[:, :, lcol:lcol + 4], cum[:, :, fcol:fcol + 4], ALU.subtract, [BG, Bc], [BBK_])
            act(BK_, BK_, AF.Exp, [BBK_], [BBK_])
            act(DEC_, tot[:, :, fcol:fcol + 4], AF.Exp, [Btot], [BDEC_])
            gate[dname] = (A_, BA_, BK_, BBK_, DEC_, BDEC_)
        qT, BqT = tile(BF16, [128, 2, L])
        kT, BkT = tile(BF16, [128, 2, L + CL])
        v, Bv = tile(BF16, [128, 18, 256])
        ot, Bot = tile(BF16, [128, 16, 256])
        ktok, Bktok = tile(BF16, [128, 18, 256])
        vt = {"f": tile(BF16, [128, 18, 264]), "b": tile(BF16, [128, 18, 264])}
        SS = {"f": tile(BF16, [128, 16, 128]), "b": tile(BF16, [128, 16, 128])}
        raw = {"f": tile(F32, [128, 16, 264]), "b": tile(F32, [128, 16, 264])}
        Tst = {"f": [tile(F32, [128, 2, 264]), tile(F32, [128, 2, 264])], "b": [tile(F32, [128, 2, 264]), tile(F32, [128, 2, 264])]}
        Cr = {"f": ring(2, BF16, [128, 2, 264]), "b": ring(2, BF16, [128, 2, 264])}
        hsum, Bhs = tile(F32, [128, 16, 256])
        hmtok, Bhmt = tile(BF16, [128, 16, 256])
        hmT, BhmT = tile(BF16, [128, 2, L])
        sm = {k_: tile(F32, [128, 16]) for k_ in ("den", "c_f", "c_b", "sum", "ssq", "mean", "rstd", "t0", "nmr")}
        sfr = ring(3, F32, [128, 128])
        for h in range(NH):
            for dc in range(2):
                P.dma("sp", lambda e, dc=dc, h=h: e.dma_start(out=qT[:, dc, :], in_=fmT_s[8 + 2 * h + dc]), BqT, writes=[BqT])
                P.dma("sp", lambda e, dc=dc, h=h: e.dma_start(out=kT[:, dc, 0:L], in_=fmT_s[2 * h + dc]), BkT, writes=[BkT])
                P.dma("sp", lambda e, dc=dc, h=h: e.dma_start(out=kT[:, dc, L:L + CL], in_=kTc_s[2 * h + dc]), BkT, writes=[BkT])
            P.dma("sp", lambda e, h=h: e.dma_start(out=v[:, 0:16, :], in_=v_s[h]), Bv, writes=[Bv])
            P.dma("sp", lambda e, h=h: e.dma_start(out=v[:, 16:18, :], in_=vc_s[h]), Bv, writes=[Bv])
            ld(ot, o_s[h], Bot)
            tt("pool", ot, ot, rv("mng", h * 256, 256).unsqueeze(1).to_broadcast([128, 16, 256]), ALU.mult, [Bot, Browv], [Bot])
            for dname in ("f", "b"):
                A_, BA_, BK_, BBK_, DEC_, BDEC_ = gate[dname]
                vtt, Bvt = vt[dname]
                veng = "dve" if dname == "f" else "pool"
                tt(veng, vtt[:, :, 0:256], v, BK_[:, :, h:h + 1].to_broadcast([128, 18, 256]), ALU.mult, [Bv, BBK_], [Bvt])
                cp(veng, vtt[:, :, 256:257], BK_[:, :, h:h + 1], [BBK_], [Bvt])
            for ch in range(18):
                ps, Bp = PS()
                pb = ps[:, 0:128].bitcast(BF16)
                for dc in range(2):
                    tr(pb[:, dc * 128:(dc + 1) * 128], kT[:, dc, ch * 128:(ch + 1) * 128], [BkT, Bident], [Bp])
                cp("act" if ch % 2 else "dve", ktok[:, ch, :], pb, [Bp], [Bktok])
            for ch in range(16):
                csl = slice(ch * 128, (ch + 1) * 128)
                ps, Bp = PS()
                for dc in range(2):
                    mm(ps[:, 0:128], kT[:, dc, csl], qT[:, dc, csl], dc == 0, dc == 1, [BkT, BqT], [Bp])
                sfu, Bsfu = sfr.get()
                cp("act", sfu, ps[:, 0:128], [Bp], [Bsfu])
                P.op("pool", lambda e, o_=SS["f"][0][:, ch, :], i_=sfu: e.affine_select(out=o_, in_=i_, pattern=[[1, 128]], compare_op=ALU.is_ge, fill=0.0, base=0, channel_multiplier=-1), [Bsfu], [SS["f"][1]])
                P.op("pool", lambda e, o_=SS["b"][0][:, ch, :], i_=sfu: e.affine_select(out=o_, in_=i_, pattern=[[-1, 128]], compare_op=ALU.is_ge, fill=0.0, base=0, channel_multiplier=1), [Bsfu], [SS["b"][1]])
            order = {"f": [16, 17] + list(range(16)), "b": [17, 16] + list(range(15, -1, -1))}
            cur = {"f": 0, "b": 0}
            upend = {}

            def emit_U(i):
                for dname in ("f", "b"):
                    vtt, Bvt = vt[dname]
                    ch = order[dname][i]
                    lst = []
                    for kc in range(2):
                        psu, Bpu = PS()
                        mm(psu[:, 0:257], ktok[:, ch, kc * 128:(kc + 1) * 128], vtt[:, ch, 0:257], True, True, [Bktok, Bvt], [Bpu])
                        lst.append((psu, Bpu))
                    upend[(dname, i)] = lst

            emit_U(0)
            rawpend = []
            for i in range(18):
                for dname in ("f", "b"):
                    A_, BA_, BK_, BBK_, DEC_, BDEC_ = gate[dname]
                    Tc, BTc = Tst[dname][cur[dname]]
                    Tn, BTn = Tst[dname][1 - cur[dname]]
                    prev = order[dname][i - 1] if i > 0 else None
                    if i < 17:
                        for kc in range(2):
                            psu, Bpu = upend[(dname, i)][kc]
                            if i == 0:
                                cp("dve", Tn[:, kc, 0:257], psu[:, 0:257], [Bpu], [BTn])
                            else:
                                stt("dve", Tn[:, kc, 0:257], Tc[:, kc, 0:257], DEC_[:, prev, h:h + 1], psu[:, 0:257], ALU.mult, ALU.add, [BTc, BDEC_, Bpu], [BTn])
                if i + 1 < 17:
                    emit_U(i + 1)
                newpend = []
                for dname in ("f", "b"):
                    A_, BA_, BK_, BBK_, DEC_, BDEC_ = gate[dname]
                    vtt, Bvt = vt[dname]
                    Tc, BTc = Tst[dname][cur[dname]]
                    ch = order[dname][i]
                    prev = order[dname][i - 1] if i > 0 else None
                    if i >= 2:
                        Cb, BCb = Cr[dname].get()
                        act(Cb, Tc, AF.Copy, [BTc, BDEC_], [BCb], scale=DEC_[:, prev, h:h + 1])
                        pso, Bpo = PS()
                        csl = slice(ch * 128, (ch + 1) * 128)
                        mm(pso[:, 0:257], SS[dname][0][:, ch, :], vtt[:, ch, 0:257], True, False, [SS[dname][1], Bvt], [Bpo])
                        for kc in range(2):
                            mm(pso[:, 0:257], qT[:, kc, csl], Cb[:, kc, 0:257], False, kc == 1, [BqT, BCb], [Bpo])
                        newpend.append((dname, ch, pso, Bpo))
                    cur[dname] = 1 - cur[dname]
                for dname, ch_, pso_, Bpo_ in rawpend:
                    cp("act", raw[dname][0][:, ch_, 0:257], pso_[:, 0:257], [Bpo_], [raw[dname][1]])
                rawpend = newpend
            for dname, ch_, pso_, Bpo_ in rawpend:
                cp("act", raw[dname][0][:, ch_, 0:257], pso_[:, 0:257], [Bpo_], [raw[dname][1]])
            for dname in ("f", "b"):
                A_, BA_ = gate[dname][0], gate[dname][1]
                rw, Brw = raw[dname]
                den, Bden = sm["den"]
                cc_, Bcc = sm["c_" + dname]
                t0, Bt0 = sm["t0"]
                tt("dve", den, rw[:, :, 256], A_[:, 0:16, h], ALU.mult, [Brw, BA_], [Bden])
                stt("dve", t0, den, -1.0, den, ALU.mult, ALU.max, [Bden], [Bt0])
                ts("dve", t0, t0, 16.0, ALU.max, [Bt0], [Bt0])
                P.op("dve", lambda e, t0=t0: e.reciprocal(out=t0, in_=t0), [Bt0], [Bt0])
                tt("dve", cc_, t0, A_[:, 0:16, h], ALU.mult, [Bt0, BA_], [Bcc])
            tt("dve", hsum, raw["f"][0][:, :, 0:256], sm["c_f"][0].unsqueeze(2).to_broadcast([128, 16, 256]), ALU.mult, [raw["f"][1], sm["c_f"][1]], [Bhs])
            rb_ = raw["b"][0][:, :, 0:256]
            tt("pool", rb_, rb_, sm["c_b"][0].unsqueeze(2).to_broadcast([128, 16, 256]), ALU.mult, [raw["b"][1], sm["c_b"][1]], [raw["b"][1]])
            tt("dve", hsum, hsum, rb_, ALU.add, [Bhs, raw["b"][1]], [Bhs])
            tmpb, Btmpb = raw["f"][0][:, :, 0:256], raw["f"][1]
            sum_, Bsum = sm["sum"]
            ssq, Bssq = sm["ssq"]
            mean, Bmean = sm["mean"]
            rstd, Brstd = sm["rstd"]
            P.op("dve", lambda e: e.tensor_reduce(out=sum_, in_=hsum, axis=AX.X, op=ALU.add), [Bhs], [Bsum])
            act(tmpb, hsum, AF.Square, [Bhs], [Btmpb])
            P.op("dve", lambda e: e.tensor_reduce(out=ssq, in_=tmpb, axis=AX.X, op=ALU.add), [Btmpb], [Bssq])
            ts("dve", mean, sum_, 1.0 / DH, ALU.mult, [Bsum], [Bmean])
            t0, Bt0 = sm["t0"]
            tt("dve", t0, mean, mean, ALU.mult, [Bmean], [Bt0])
            stt("dve", rstd, ssq, 1.0 / DH, t0, ALU.mult, ALU.subtract, [Bssq, Bt0], [Brstd])
            act(rstd, rstd, AF.Sqrt, [Brstd, Beps], [Brstd], bias=epsT[:, 0:1])
            P.op("dve", lambda e: e.reciprocal(out=rstd, in_=rstd), [Brstd], [Brstd])
            nmr, Bnmr = sm["nmr"]
            stt("dve", nmr, mean, -1.0, rstd, ALU.mult, ALU.mult, [Bmean, Brstd], [Bnmr])
            for tc in range(16):
                act(hsum[:, tc, :], hsum[:, tc, :], AF.Identity, [Bhs, Brstd, Bnmr], [Bhs], bias=nmr[:, tc:tc + 1], scale=rstd[:, tc:tc + 1])
            tt("dve", hmtok, hsum, ot, ALU.mult, [Bhs, Bot], [Bhmt])
            for dc in range(2):
                for t4 in range(4):
                    ps, Bp = PS()
                    pb = ps[:, 0:256].bitcast(BF16)
                    for q_ in range(4):
                        tc = t4 * 4 + q_
                        tr(pb[:, q_ * 128:(q_ + 1) * 128], hmtok[:, tc, dc * 128:(dc + 1) * 128], [Bhmt, Bident], [Bp])
                    cp("act", hmT[:, dc, t4 * 512:(t4 + 1) * 512], pb, [Bp], [BhmT])
            for dc in range(2):
                P.dma("pool", lambda e, dc=dc, h=h: e.dma_start(out=hmT_s[2 * h + dc], in_=hmT[:, dc, :]), BhmT, reads=[BhmT])
        P.barrier()
        aoff[0] = m0

    def stage3():
        m0 = aoff[0]
        ftr = ring(4, BF16, [128, 16, 128])
        kr_r = ring(2, F32, [128, 512])
        ki_r = ring(2, F32, [128, 512])
        fmr = ring(2, BF16, [128, 32, 256])
        t_r = ring(8, F32, [128, 512])
        e_r = ring(3, F32, [128, 256])
        zT = [tile(BF16, [128, L]) for _ in range(4)]
        x1T, Bx1 = tile(BF16, [128, 4, L])
        x2T, Bx2 = tile(BF16, [128, 4, L])
        hh, Bhh = x2T, Bx2
        ztok, Bztok = tile(BF16, [128, 16, 512])
        Y, BY = tile(BF16, [128, 32, 512])
        for c2 in range(2):
            for i in range(4):
                ld(zT[i][0], fmT_s[16 + c2 * 4 + i], zT[i][1])
                P.dma("sp", lambda e, i=i, c2=c2: e.dma_start(out=x1T[:, i, :], in_=fmT_s[24 + c2 * 4 + i]), Bx1, writes=[Bx1])
                P.dma("sp", lambda e, i=i, c2=c2: e.dma_start(out=x2T[:, i, :], in_=fmT_s[32 + c2 * 4 + i]), Bx2, writes=[Bx2])
            for o in range(2):
                for tc in range(16):
                    ps, Bp = PS()
                    pb = ps[:, 0:256].bitcast(BF16)
                    for i in range(4):
                        tr(pb[:, i * 128:(i + 1) * 128], zT[i][0][:, tc * 128:(tc + 1) * 128], [zT[i][1], Bident], [Bp])
                    cp("act" if tc % 2 else "dve", ztok[:, tc, :], pb, [Bp], [Bztok])
                for fi in range(16):
                    fre, Bfre = ftr.get()
                    ld(fre, FTs[fi], Bfre)
                    fim, Bfim = ftr.get()
                    ld(fim, FTs[16 + fi], Bfim)
                    kr, Bkr = kr_r.get()
                    ld(kr, Kspec[o, 0, fi, :, c2 * 512:(c2 + 1) * 512], Bkr)
                    ki, Bki = ki_r.get()
                    ld(ki, Kspec[o, 1, fi, :, c2 * 512:(c2 + 1) * 512], Bki)
                    psR, BpR = PS()
                    for jc in range(16):
                        mm(psR[:, :], fre[:, jc, :], ztok[:, jc, :], jc == 0, jc == 15, [Bfre, Bztok], [BpR])
                    psI, BpI = PS()
                    for jc in range(16):
                        mm(psI[:, :], fim[:, jc, :], ztok[:, jc, :], jc == 0, jc == 15, [Bfim, Bztok], [BpI])
                    t1, Bt1 = t_r.get()
                    t2, Bt2 = t_r.get()
                    tt("dve", t1, psR[:, :], kr, ALU.mult, [BpR, Bkr], [Bt1])
                    tt("dve", t2, psI[:, :], ki, ALU.mult, [BpI, Bki], [Bt2])
                    tt("pool", Y[:, fi, :], t1, t2, ALU.subtract, [Bt1, Bt2], [BY])
                    t3, Bt3 = t_r.get()
                    t4_, Bt4 = t_r.get()
                    tt("dve", t3, psR[:, :], ki, ALU.mult, [BpR, Bki], [Bt3])
                    tt("dve", t4_, psI[:, :], kr, ALU.mult, [BpI, Bkr], [Bt4])
                    tt("pool", Y[:, 16 + fi, :], t3, t4_, ALU.add, [Bt3, Bt4], [BY])
                    if fi == 0:
                        tt("dve", Y[0:1, 16, :], psI[0:1, :], knyq[0:1, o, c2 * 512:(c2 + 1) * 512], ALU.mult, [BpI, Bknyq, BY], [BY])
                for t8 in range(8):
                    fm, Bfm = fmr.get()
                    ld(fm, Fms[t8], Bfm)
                    tsl = slice(t8 * 256, (t8 + 1) * 256)
                    for i in range(4):
                        ps, Bp = PS()
                        for rc in range(32):
                            mm(ps[:, 0:256], Y[:, rc, i * 128:(i + 1) * 128], fm[:, rc, :], rc == 0, rc == 31, [BY, Bfm], [Bp])
                        ev, Bev = e_r.get()
                        skc = COLV["skip"][0] + o * 8 + c2 * 4 + i
                        stt("dve", ev, zT[i][0][:, tsl], colv[:, skc:skc + 1], ps[:, 0:256], ALU.mult, ALU.add, [zT[i][1], Bcolv, Bp], [Bev])
                        if o == 0:
                            tt("pool", zT[i][0][:, tsl], ev, x1T[:, i, tsl], ALU.mult, [Bev, Bx1], [zT[i][1]])
                        else:
                            tt("pool", hh[:, i, tsl], ev, x2T[:, i, tsl], ALU.mult, [Bev, Bx2], [Bhh])
            for i in range(4):
                P.dma("pool", lambda e, i=i, c2=c2: e.dma_start(out=hhT_s[c2 * 4 + i], in_=hh[:, i, :]), Bhh, reads=[Bhh])
        P.barrier()
        aoff[0] = m0

    out_dmas = []

    def rms_rstd(xt_, Bx_, TT, sqring, rs, Brs):
        ps, Bp = PS()
        for kc in range(8):
            sq, Bsq = sqring.get()
            act(sq, xt_[:, kc, :], AF.Square, [Bx_], [Bsq])
            mm(ps[:, 0:TT], onesf, sq, kc == 0, kc == 7, [Bones, Bsq], [Bp])
        act(rs, ps[:, 0:TT], AF.Sqrt, [Bp, Beps], [Brs], bias=epsT[:, 0:1], scale=1.0 / D)
        P.op("dve", lambda e: e.reciprocal(out=rs, in_=rs), [Brs], [Brs])

    def stage45(b, j):
        m0 = aoff[0]
        TT = 512
        NT = L // TT
        wpm, Bwpm = tile(BF16, [128, 8, D])
        wph, Bwph = tile(BF16, [128, 8, D])
        wo_, Bwo = tile(BF16, [128, 8, D])
        ld(wpm, Wpm[:, :, :], Bwpm)
        ld(wph, Wph[:, :, :], Bwph)
        ld(wo_, Wout[:, :, :], Bwo)
        inr = {n_: (view(BF16, [128, 8, TT]), [Buf("in") for _ in range(8)]) for n_ in ("hm", "hh", "gm", "gh")}
        xr = ring(2, F32, [128, 8, TT])
        sqr = ring(2, F32, [128, TT])
        rsr = ring(2, F32, [128, TT])
        tmr = ring(4, F32, [128, TT])
        ypr = ring(1, BF16, [128, 8, TT])
        ur = ring(1, BF16, [128, NFC, TT])
        w13r = ring(4, BF16, [128, 8, 128])
        w2r = ring(2, BF16, [128, NFC, 128])
        outr = ring(2, F32, [128, TT])
        st_ = {}

        def loads(t_):
            tsl = slice(t_ * TT, (t_ + 1) * TT)
            tl = {}
            for n_, srcd, base in (("hm", hmT_s, 0), ("hh", hhT_s, 0), ("gm", fmT_s, 40), ("gh", fmT_s, 48)):
                tl[n_] = inr[n_]
                for kc in range(8):
                    P.dma("sp", lambda e, dst=tl[n_][0], srcd=srcd, base=base, kc=kc, tsl=tsl: e.dma_start(out=dst[:, kc, :], in_=srcd[base + kc, :, tsl]),
                          tl[n_][1][kc], writes=[tl[n_][1][kc]])
            st_.setdefault(t_, {})
            st_[t_]["tl"] = tl
            st_[t_]["tsl"] = tsl

        def loadx(t_):
            tsl = st_[t_]["tsl"] if t_ in st_ else slice(t_ * TT, (t_ + 1) * TT)
            xt, Bx = xr.get()
            ld(xt, xT[b][:, tsl].rearrange("(kc p) t -> p kc t", p=128), Bx)
            st_.setdefault(t_, dict(tsl=tsl))
            st_[t_]["xt"] = xt
            st_[t_]["Bx"] = Bx

        def phaseA(t_):
            d_ = st_[t_]
            tl, xt, Bx = d_["tl"], d_["xt"], d_["Bx"]
            yp, Byp = ypr.get()
            for cc in range(8):
                csl = slice(cc * 128, (cc + 1) * 128)
                ps, Bp = PS()
                for kc in range(8):
                    mm(ps[:, :], wpm[:, kc, csl], tl["hm"][0][:, kc, :], kc == 0, kc == 7, [Bwpm, tl["hm"][1][kc]], [Bp])
                ps2, Bp2 = PS()
                for kc in range(8):
                    mm(ps2[:, :], wph[:, kc, csl], tl["hh"][0][:, kc, :], kc == 0, kc == 7, [Bwph, tl["hh"][1][kc]], [Bp2])
                tm, Btm = tmr.get()
                tt("dve", tm, ps[:, :], tl["gm"][0][:, cc, :], ALU.mult, [Bp, tl["gm"][1][cc]], [Btm])
                tm2, Btm2 = tmr.get()
                tt("dve", tm2, ps2[:, :], tl["gh"][0][:, cc, :], ALU.mult, [Bp2, tl["gh"][1][cc]], [Btm2])
                tt("pool", yp[:, cc, :], tm, tm2, ALU.add, [Btm, Btm2], [Byp])
            if t_ + 1 < NT:
                loads(t_ + 1)
            for cc in range(8):
                csl = slice(cc * 128, (cc + 1) * 128)
                ps, Bp = PS()
                for kc in range(8):
                    mm(ps[:, :], wo_[:, kc, csl], yp[:, kc, :], kc == 0, kc == 7, [Bwo, Byp], [Bp])
                stt("dve", xt[:, cc, :], ps[:, :], modT[:, 16 + cc, j:j + 1], xt[:, cc, :], ALU.mult, ALU.add, [Bp, Bmod, Bx], [Bx])
            rs, Brs = rsr.get()
            rms_rstd(xt, Bx, TT, sqr, rs, Brs)
            h2, Bh2 = yp, Byp
            for kc in range(8):
                tm, Btm = tmr.get()
                stt("dve", tm, xt[:, kc, :], s2T[:, kc, j:j + 1], rs, ALU.mult, ALU.mult, [Bx, Bs2, Brs], [Btm])
                act(h2[:, kc, :], tm, AF.Identity, [Btm, Bmod], [Bh2], bias=modT[:, 24 + kc, j:j + 1])
            d_["h2"] = (h2, Bh2)

        def phaseB(t_):
            d_ = st_[t_]
            h2, Bh2 = d_["h2"]
            if t_ + 1 < NT:
                loadx(t_ + 1)
            u, Bu = ur.get()
            for fc in range(NFC):
                wa, Bwa = w13r.get()
                ld(wa, W1[fc], Bwa)
                wb, Bwb = w13r.get()
                ld(wb, W3[fc], Bwb)
                psa, Bpa = PS()
                for kc in range(8):
                    mm(psa[:, :], wa[:, kc, :], h2[:, kc, :], kc == 0, kc == 7, [Bwa, Bh2], [Bpa])
                psb, Bpb = PS()
                for kc in range(8):
                    mm(psb[:, :], wb[:, kc, :], h2[:, kc, :], kc == 0, kc == 7, [Bwb, Bh2], [Bpb])
                tm, Btm = tmr.get()
                act(tm, psa[:, :], AF.Silu, [Bpa], [Btm])
                tt("dve", u[:, fc, :], psb[:, :], tm, ALU.mult, [Bpb, Btm], [Bu])
            d_["u"] = (u, Bu)

        def phaseC(t_):
            d_ = st_[t_]
            xt, Bx, tsl = d_["xt"], d_["Bx"], d_["tsl"]
            u, Bu = d_["u"]
            for cc in range(8):
                w2t_, Bw2t = w2r.get()
                ld(w2t_, W2[cc], Bw2t)
                ps, Bp = PS()
                for fc in range(NFC):
                    mm(ps[:, :], w2t_[:, fc, :], u[:, fc, :], fc == 0, fc == NFC - 1, [Bw2t, Bu], [Bp])
                stt("dve", xt[:, cc, :], ps[:, :], modT[:, 40 + cc, j:j + 1], xt[:, cc, :], ALU.mult, ALU.add, [Bp, Bmod, Bx], [Bx])
            rs, Brs = rsr.get()
            rms_rstd(xt, Bx, TT, sqr, rs, Brs)
            for cc in range(8):
                ot_, Bo_ = outr.get()
                stt("dve", ot_, xt[:, cc, :], cv("final_g", cc), rs, ALU.mult, ALU.mult, [Bx, Bcolv, Brs], [Bo_])
                out_dmas.append(stq(outT[b, cc * 128:(cc + 1) * 128, tsl], ot_, Bo_, [Bo_]))

        loads(0)
        loadx(0)
        phaseA(0)
        for t_ in range(NT):
            phaseB(t_)
            if t_ + 1 < NT:
                phaseA(t_ + 1)
            phaseC(t_)
        P.barrier()
        aoff[0] = m0

    flush_casts(10 ** 6)
    P.barrier(include_deferred=True)
    for b in range(NB):
        if stop_after >= 2:
            stage1(ctxT[b], CL, 4, True)
            stage1(xT[b], L, b, False)
        if stop_after >= 3:
            stage2()
        if stop_after >= 4:
            stage3()
        if stop_after >= 5:
            stage45(b, b)
    P.barrier()
    P.emit(final_wait_ops=out_dmas[-8:])
    st.close()
    return nc


def _cols(v):
    v = np.asarray(v, np.float32).reshape(-1, 128)
    return np.ascontiguousarray(v.T)


def host_constants():
    n = L
    N = NFFT
    s = np.arange(n, dtype=np.float64)
    f = np.arange(n, dtype=np.float64)
    ang = 2.0 * np.pi * np.outer(s, f) / N
    FT = np.empty((n, N), np.float64)
    FT[:, :n] = np.cos(ang)
    FT[:, n:] = -np.sin(ang)
    FT[:, n] = np.cos(np.pi * s)
    FTb = FT.astype(ml_dtypes.bfloat16)
    FTs = np.ascontiguousarray(FTb.reshape(16, 128, 32, 128).transpose(2, 1, 0, 3))
    Fm = FTb.T
    Fms = np.ascontiguousarray(Fm.reshape(32, 128, 8, 256).transpose(2, 1, 0, 3))
    t = np.linspace(0.0, 1.0, n, dtype=np.float32)[:, None]
    bands = np.linspace(1e-4, 15.0, 16, dtype=np.float32)
    angz = (np.float32(2.0 * math.pi / n)) * np.arange(n, dtype=np.float32)[:, None] * bands[None, :]
    z = np.concatenate([t, np.cos(angz), -np.sin(angz)], axis=-1).astype(np.float32)
    zT = np.ascontiguousarray(z.T)
    tvec = np.ascontiguousarray(t[:, 0].reshape(16, 128).T)
    return FTs, Fms, zT, tvec


_CONST_CACHE = {}


def make_in_maps(inputs, NB=NB_FULL, ncores=NCORES):
    if "c" not in _CONST_CACHE:
        _CONST_CACHE["c"] = host_constants()
    FTs, Fms, zT, tvec = _CONST_CACHE["c"]
    g = {k: np.asarray(v) for k, v in inputs.items()}
    b_in = g["b_in"][0]
    colv = np.zeros((128, NCOLV), np.float32)

    def put(name, arr):
        o, w = COLV[name]
        assert arr.shape == (128, w), (name, arr.shape)
        colv[:, o:o + w] = arr

    put("b_ada", _cols(g["b_ada"][0]))
    put("norm1_g", _cols(g["norm1_g"][0]))
    put("norm2_g", _cols(g["norm2_g"][0]))
    put("final_g", _cols(g["final_g"]))
    put("b_fm", np.concatenate([_cols(b_in[fm_src_col(cc):fm_src_col(cc) + 128]) for cc in range(56)], axis=1))
    convw = np.concatenate([g["kq_conv_w"][0][:, :1024], g["kq_conv_w"][0][:, 1024:], g["hy_conv_w"][0]], axis=1)
    convb = np.concatenate([g["kq_conv_b"][0][:1024], g["kq_conv_b"][0][1024:], g["hy_conv_b"][0]])
    cw = np.zeros((128, 40, 3), np.float32)
    for jt in range(3):
        cw[:, :, jt] = _cols(convw[jt])
    put("cw", cw.reshape(128, 120))
    put("cb", _cols(convb))
    put("skip", _cols(g["hy_skip"][0].reshape(-1)))
    rowv = np.zeros((NROWV,), np.float32)

    def putr(name, arr):
        o, w = ROWV[name]
        rowv[o:o + w] = arr

    putr("bv", b_in[V_OFF:V_OFF + 1024])
    putr("bo", b_in[O_OFF:O_OFF + 1024])
    putr("bg", b_in[G_OFF:G_OFF + 16])
    putr("mng", g["m_norm_g"][0])
    rowv = np.ascontiguousarray(np.broadcast_to(rowv[None, :], (128, NROWV)))
    hyv = np.stack([g["hy_b1"][0], g["hy_b2"][0], g["hy_b3"][0], g["hy_freq"][0]], axis=1).astype(np.float32)
    shared = {
        "w_ada": g["w_ada"][0], "w_in": g["w_in"][0], "w_pm": g["w_pm"][0], "w_ph": g["w_ph"][0],
        "w_out": g["w_out"][0], "ffn_w1": g["ffn_w1"][0], "ffn_w3": g["ffn_w3"][0], "ffn_w2": g["ffn_w2"][0],
        "hy_w1": g["hy_w1"][0], "hy_w2": g["hy_w2"][0], "hy_w3": g["hy_w3"][0], "hy_w4": g["hy_w4"][0],
        "hyv": hyv, "decay_bc": np.broadcast_to(g["hy_decay"][0].reshape(1, -1), (128, 4096)), "colv": colv, "rowv": rowv, "FTs": FTs, "Fms": Fms, "zT": zT, "tvec": tvec,
    }
    shared = {k: np.ascontiguousarray(v) for k, v in shared.items()}
    maps = []
    for c in range(ncores):
        b0 = c * NB
        m = dict(shared)
        m["xT"] = np.ascontiguousarray(g["x"][b0:b0 + NB].transpose(0, 2, 1))
        m["ctxT"] = np.ascontiguousarray(g["ctx"][b0:b0 + NB].transpose(0, 2, 1))
        cc_ = np.zeros((5, D), np.float32)
        cc_[:NB] = g["c"][b0:b0 + NB]
        cc_[4] = g["c_ctx"]
        m["cT"] = np.ascontiguousarray(cc_.reshape(5, 8, 128).transpose(2, 1, 0))
        maps.append(m)
    return maps


_NC_CACHE = {}


def kernel(**inputs):
    if "nc" not in _NC_CACHE:
        _NC_CACHE["nc"] = build_program(NB_FULL)
    nc = _NC_CACHE["nc"]
    maps = make_in_maps(inputs)
    res = run_bass_kernel_spmd(nc, maps, core_ids=list(range(NCORES)))
    outs = [np.asarray(r["outT"]) for r in res.results]
    full = np.concatenate(outs, axis=0).transpose(0, 2, 1)
    return np.ascontiguousarray(full.astype(np.float32))
```
